# Optimizing a Trainium2 kernel written in Bass

```python
import math
import jax
import jax.numpy as jnp
from jax import lax
import numpy as np

D_MODEL = 1024
BATCH = 4
SEQ = 4096
DEPTH = 1
DEC_BATCH = 16
DEC_SEQ = 32
PAST_LEN = 2048

CHUNK = 64
N_META = 16
N_HEADS = 8
QK_NOPE = 128
QK_ROPE = 64
V_DIM = 128
QK_DIM = QK_NOPE + QK_ROPE
Q_LORA = 384
KV_LORA = 256
D_SSM = D_MODEL
SSM_GROUP = 16
N_GROUPS = D_SSM // SSM_GROUP
SSM_STATE = 64
D_FF = 4 * D_MODEL
Q_BLOCK = 128
ROPE_THETA = 10000.0
EPS = 1e-6
ATTN_SCALE = QK_DIM ** -0.5
NEG_INF = -1e30

O_KV = Q_LORA
O_KR = O_KV + KV_LORA
O_SSM = O_KR + QK_ROPE
O_GA = O_SSM + D_SSM
O_GB = O_GA + D_MODEL
IN_COLS = O_GB + D_MODEL

kernel_name = "mla_s5_gated_streaming_encoder_step"


def rmsnorm(x, g):
    xf = x.astype(jnp.float32)
    y = xf * lax.rsqrt(jnp.mean(xf * xf, axis=-1, keepdims=True) + EPS)
    return (y * g.astype(jnp.float32)).astype(x.dtype)


def rope(x, pos):
    half = QK_ROPE // 2
    inv = ROPE_THETA ** (-jnp.arange(half, dtype=jnp.float32) / half)
    ang = pos.astype(jnp.float32)[:, None] * inv[None, :]
    ang = ang.reshape((ang.shape[0],) + (1,) * (x.ndim - 3) + (half,))
    cos, sin = jnp.cos(ang), jnp.sin(ang)
    xf = x.astype(jnp.float32)
    x1, x2 = xf[..., :half], xf[..., half:]
    return jnp.concatenate([x1 * cos - x2 * sin, x2 * cos + x1 * sin], axis=-1).astype(x.dtype)


def attend(q, k, v, mask):
    s = jnp.einsum("bqhd,bkhd->bhqk", q.astype(jnp.float32), k.astype(jnp.float32)) * ATTN_SCALE
    if mask is not None:
        s = jnp.where(mask[None, None], s, NEG_INF)
    p = jax.nn.softmax(s, axis=-1)
    return jnp.einsum("bhqk,bkhv->bqhv", p, v.astype(jnp.float32)).astype(v.dtype)


def mla_expand(c_kv, k_rope, p):
    kv = jnp.einsum("bkc,chd->bkhd", c_kv, p["w_ukv"])
    k_nope = rmsnorm(kv[..., :QK_NOPE], p["k_nope_norm"])
    v = kv[..., QK_NOPE:]
    kr = jnp.broadcast_to(k_rope[:, :, None, :], k_nope.shape[:3] + (QK_ROPE,)).astype(k_nope.dtype)
    return jnp.concatenate([k_nope, kr], axis=-1), v


def block_causal_attention(q, k, v):
    bsz, L = q.shape[0], q.shape[1]
    n_real = L - N_META
    o_meta = attend(q[:, :N_META], k[:, :N_META], v[:, :N_META], None)
    key_chunk = jnp.concatenate([jnp.full((N_META,), -1, jnp.int32),
                                 jnp.arange(n_real, dtype=jnp.int32) // CHUNK])
    n_blk = n_real // Q_BLOCK
    q_blocks = q[:, N_META:].reshape(bsz, n_blk, Q_BLOCK, N_HEADS, QK_DIM).transpose(1, 0, 2, 3, 4)

    def one_block(args):
        qb, i = args
        q_chunk = (i * Q_BLOCK + jnp.arange(Q_BLOCK, dtype=jnp.int32)) // CHUNK
        mask = key_chunk[None, :] <= q_chunk[:, None]
        return attend(qb, k, v, mask)

    o = lax.map(one_block, (q_blocks, jnp.arange(n_blk, dtype=jnp.int32)))
    o = o.transpose(1, 0, 2, 3, 4).reshape(bsz, n_real, N_HEADS, V_DIM)
    return jnp.concatenate([o_meta, o], axis=1)


def complex_affine_combine(e1, e2):
    a1r, a1i, b1r, b1i = e1
    a2r, a2i, b2r, b2i = e2
    return (a2r * a1r - a2i * a1i,
            a2r * a1i + a2i * a1r,
            a2r * b1r - a2i * b1i + b2r,
            a2r * b1i + a2i * b1r + b2i)


def s5_scan(u, h0_re, h0_im, p):
    f32 = jnp.float32
    a_re = p["ssm_a_re"].astype(f32)
    a_im = p["ssm_a_im"].astype(f32)
    dt = jnp.exp(p["ssm_log_dt"].astype(f32))[:, None]
    mag = jnp.exp(a_re * dt)
    lam_re, lam_im = mag * jnp.cos(a_im * dt), mag * jnp.sin(a_im * dt)
    den = a_re * a_re + a_im * a_im
    f_re = ((lam_re - 1.0) * a_re + lam_im * a_im) / den
    f_im = (lam_im * a_re - (lam_re - 1.0) * a_im) / den
    bsz, L = u.shape[0], u.shape[1]
    ug = u.astype(f32).reshape(bsz, L, N_GROUPS, SSM_GROUP)
    bu_re = jnp.einsum("blgc,gnc->blgn", ug, p["ssm_b_re"].astype(f32))
    bu_im = jnp.einsum("blgc,gnc->blgn", ug, p["ssm_b_im"].astype(f32))
    x_re = f_re * bu_re - f_im * bu_im
    x_im = f_re * bu_im + f_im * bu_re
    ar = jnp.broadcast_to(lam_re, x_re.shape)
    ai = jnp.broadcast_to(lam_im, x_re.shape)
    ar, ai, sr, si = lax.associative_scan(complex_affine_combine, (ar, ai, x_re, x_im), axis=1)
    h0r = h0_re.astype(f32)[:, None]
    h0i = h0_im.astype(f32)[:, None]
    hr = sr + ar * h0r - ai * h0i
    hi = si + ar * h0i + ai * h0r
    y = (jnp.einsum("blgn,gcn->blgc", hr, p["ssm_c_re"].astype(f32))
         - jnp.einsum("blgn,gcn->blgc", hi, p["ssm_c_im"].astype(f32)))
    y = y.reshape(bsz, L, D_SSM) + p["ssm_d"].astype(f32) * u.astype(f32)
    return y, hr[:, -1], hi[:, -1]


def hybrid_layer(h, pos, prefix, h0_re, h0_im, p):
    xn = rmsnorm(h, p["norm_mix"])
    z = jnp.einsum("bld,dc->blc", xn, p["w_in"])
    q_lat, kv_lat, kr_raw = z[..., :O_KV], z[..., O_KV:O_KR], z[..., O_KR:O_SSM]
    u, g_a, g_b = z[..., O_SSM:O_GA], z[..., O_GA:O_GB], z[..., O_GB:]
    c_q = rmsnorm(q_lat, p["q_lora_norm"])
    q = jnp.einsum("blr,rhd->blhd", c_q, p["w_uq"])
    q = jnp.concatenate([rmsnorm(q[..., :QK_NOPE], p["q_nope_norm"]),
                         rope(rmsnorm(q[..., QK_NOPE:], p["q_rope_norm"]), pos)], axis=-1)
    c_kv = rmsnorm(kv_lat, p["kv_lora_norm"])
    k_rope = rope(rmsnorm(kr_raw, p["k_rope_norm"]), pos)
    if prefix is None:
        k, v = mla_expand(c_kv, k_rope, p)
        o_attn = block_causal_attention(q, k, v)
    else:
        pre_c, pre_kr = prefix
        k, v = mla_expand(jnp.concatenate([pre_c, c_kv], axis=1),
                          jnp.concatenate([pre_kr, k_rope], axis=1), p)
        o_attn = attend(q, k, v, None)
    o_a = jnp.einsum("blhv,hvd->bld", o_attn, p["w_o_attn"])
    y_ssm, hr, hi = s5_scan(u, h0_re, h0_im, p)
    s = jax.nn.gelu(y_ssm)
    o_b = (s @ p["w_glu_v"]) * jax.nn.sigmoid(s @ p["w_glu_g"])
    merged = jax.nn.sigmoid(g_a) * o_a + jax.nn.sigmoid(g_b) * o_b
    h = h + merged @ p["w_out"]
    a = jax.nn.relu(rmsnorm(h, p["norm_mlp"]) @ p["w_mlp_up"])
    h = h + (a * a) @ p["w_mlp_down"]
    return h, c_kv, k_rope, hr, hi


def setup_inputs(seed: int = 0) -> dict:
    key = jax.random.key(seed)
    ks = jax.random.split(key, 40)
    nrm = jax.random.normal
    f32 = jnp.float32

    def gain(k, shape):
        return 1.0 + 0.02 * nrm(k, shape, f32)

    n_idx = jnp.arange(SSM_STATE, dtype=f32)
    return {
        "x_prompt": nrm(ks[0], (BATCH, SEQ, D_MODEL), f32),
        "x_sample": nrm(ks[1], (DEC_BATCH, DEC_SEQ, D_MODEL), f32),
        "cache_latent": nrm(ks[2], (DEPTH, DEC_BATCH, PAST_LEN, KV_LORA), f32),
        "cache_krope": nrm(ks[3], (DEPTH, DEC_BATCH, PAST_LEN, QK_ROPE), f32),
        "cache_meta_latent": nrm(ks[4], (DEPTH, DEC_BATCH, N_META, KV_LORA), f32),
        "cache_meta_krope": nrm(ks[5], (DEPTH, DEC_BATCH, N_META, QK_ROPE), f32),
        "state_ssm_re": 0.5 * nrm(ks[6], (DEPTH, DEC_BATCH, N_GROUPS, SSM_STATE), f32),
        "state_ssm_im": 0.5 * nrm(ks[7], (DEPTH, DEC_BATCH, N_GROUPS, SSM_STATE), f32),
        "meta_tokens": nrm(ks[8], (N_META, D_MODEL), f32),
        "norm_mix": gain(ks[9], (DEPTH, D_MODEL)),
        "w_in": nrm(ks[10], (DEPTH, D_MODEL, IN_COLS), f32) * D_MODEL ** -0.5,
        "q_lora_norm": gain(ks[11], (DEPTH, Q_LORA)),
        "w_uq": nrm(ks[12], (DEPTH, Q_LORA, N_HEADS, QK_DIM), f32) * Q_LORA ** -0.5,
        "q_nope_norm": gain(ks[13], (DEPTH, QK_NOPE)),
        "q_rope_norm": gain(ks[14], (DEPTH, QK_ROPE)),
        "kv_lora_norm": gain(ks[15], (DEPTH, KV_LORA)),
        "k_rope_norm": gain(ks[16], (DEPTH, QK_ROPE)),
        "w_ukv": nrm(ks[17], (DEPTH, KV_LORA, N_HEADS, QK_NOPE + V_DIM), f32) * KV_LORA ** -0.5,
        "k_nope_norm": gain(ks[18], (DEPTH, QK_NOPE)),
        "w_o_attn": nrm(ks[19], (DEPTH, N_HEADS, V_DIM, D_MODEL), f32) * (N_HEADS * V_DIM) ** -0.5,
        "ssm_a_re": -0.5 + 0.01 * nrm(ks[20], (DEPTH, N_GROUPS, SSM_STATE), f32),
        "ssm_a_im": math.pi * n_idx + 0.01 * nrm(ks[21], (DEPTH, N_GROUPS, SSM_STATE), f32),
        "ssm_log_dt": jax.random.uniform(ks[22], (DEPTH, N_GROUPS), f32, math.log(1e-3), math.log(1e-1)),
        "ssm_b_re": nrm(ks[23], (DEPTH, N_GROUPS, SSM_STATE, SSM_GROUP), f32) * (2 * SSM_GROUP) ** -0.5,
        "ssm_b_im": nrm(ks[24], (DEPTH, N_GROUPS, SSM_STATE, SSM_GROUP), f32) * (2 * SSM_GROUP) ** -0.5,
        "ssm_c_re": nrm(ks[25], (DEPTH, N_GROUPS, SSM_GROUP, SSM_STATE), f32) * SSM_STATE ** -0.5,
        "ssm_c_im": nrm(ks[26], (DEPTH, N_GROUPS, SSM_GROUP, SSM_STATE), f32) * SSM_STATE ** -0.5,
        "ssm_d": nrm(ks[27], (DEPTH, D_SSM), f32),
        "w_glu_v": nrm(ks[28], (DEPTH, D_SSM, D_MODEL), f32) * D_SSM ** -0.5,
        "w_glu_g": nrm(ks[29], (DEPTH, D_SSM, D_MODEL), f32) * D_SSM ** -0.5,
        "w_out": nrm(ks[30], (DEPTH, D_MODEL, D_MODEL), f32) * D_MODEL ** -0.5,
        "norm_mlp": gain(ks[31], (DEPTH, D_MODEL)),
        "w_mlp_up": nrm(ks[32], (DEPTH, D_MODEL, D_FF), f32) * D_MODEL ** -0.5,
        "w_mlp_down": nrm(ks[33], (DEPTH, D_FF, D_MODEL), f32) * D_FF ** -0.5,
    }


def reference(x_prompt, x_sample, cache_latent, cache_krope, cache_meta_latent, cache_meta_krope,
              state_ssm_re, state_ssm_im, meta_tokens, norm_mix, w_in, q_lora_norm, w_uq,
              q_nope_norm, q_rope_norm, kv_lora_norm, k_rope_norm, w_ukv, k_nope_norm, w_o_attn,
              ssm_a_re, ssm_a_im, ssm_log_dt, ssm_b_re, ssm_b_im, ssm_c_re, ssm_c_im, ssm_d,
              w_glu_v, w_glu_g, w_out, norm_mlp, w_mlp_up, w_mlp_down):
    bsz_p, seq_p = x_prompt.shape[0], x_prompt.shape[1]
    seq_s = x_sample.shape[1]
    meta = jnp.broadcast_to(meta_tokens[None].astype(x_prompt.dtype), (bsz_p, N_META, D_MODEL))
    hp = jnp.concatenate([meta, x_prompt], axis=1)
    pos_p = jnp.arange(N_META + seq_p, dtype=jnp.int32) - N_META
    hs = x_sample
    pos_s = PAST_LEN + jnp.arange(seq_s, dtype=jnp.int32)
    h0_zero = jnp.zeros((bsz_p, N_GROUPS, SSM_STATE), jnp.float32)

    lat_p, kr_p, mlat_p, mkr_p, sre_p, sim_p = [], [], [], [], [], []
    lat_s, kr_s, sre_s, sim_s = [], [], [], []
    for l in range(DEPTH):
        p = dict(norm_mix=norm_mix[l], w_in=w_in[l], q_lora_norm=q_lora_norm[l], w_uq=w_uq[l],
                 q_nope_norm=q_nope_norm[l], q_rope_norm=q_rope_norm[l], kv_lora_norm=kv_lora_norm[l],
                 k_rope_norm=k_rope_norm[l], w_ukv=w_ukv[l], k_nope_norm=k_nope_norm[l],
                 w_o_attn=w_o_attn[l], ssm_a_re=ssm_a_re[l], ssm_a_im=ssm_a_im[l],
                 ssm_log_dt=ssm_log_dt[l], ssm_b_re=ssm_b_re[l], ssm_b_im=ssm_b_im[l],
                 ssm_c_re=ssm_c_re[l], ssm_c_im=ssm_c_im[l], ssm_d=ssm_d[l], w_glu_v=w_glu_v[l],
                 w_glu_g=w_glu_g[l], w_out=w_out[l], norm_mlp=norm_mlp[l], w_mlp_up=w_mlp_up[l],
                 w_mlp_down=w_mlp_down[l])
        hp, c_kv, k_rope, hr, hi = hybrid_layer(hp, pos_p, None, h0_zero, h0_zero, p)
        mlat_p.append(c_kv[:, :N_META])
        mkr_p.append(k_rope[:, :N_META])
        lat_p.append(c_kv[:, N_META:])
        kr_p.append(k_rope[:, N_META:])
        sre_p.append(hr)
        sim_p.append(hi)
        prefix = (jnp.concatenate([cache_meta_latent[l], cache_latent[l]], axis=1),
                  jnp.concatenate([cache_meta_krope[l], cache_krope[l]], axis=1))
        hs, c_kv, k_rope, hr, hi = hybrid_layer(hs, pos_s, prefix, state_ssm_re[l], state_ssm_im[l], p)
        lat_s.append(c_kv)
        kr_s.append(k_rope)
        sre_s.append(hr)
        sim_s.append(hi)

    y_prompt = hp[:, N_META:]
    y_sample = hs
    return (y_prompt, y_sample,
            jnp.stack(lat_p), jnp.stack(kr_p), jnp.stack(mlat_p), jnp.stack(mkr_p),
            jnp.stack(sre_p), jnp.stack(sim_p),
            jnp.stack(lat_s), jnp.stack(kr_s), jnp.stack(sre_s), jnp.stack(sim_s))
```

```python
import math
from contextlib import ExitStack
import numpy as np
import concourse.bass as bass
import concourse.mybir as mybir
from concourse.bass_utils import run_bass_kernel_spmd

F32 = mybir.dt.float32
BF16 = mybir.dt.bfloat16
AF = mybir.ActivationFunctionType
ALU = mybir.AluOpType
AX = mybir.AxisListType

D = 1024
NOWN = 2048
NF = 2128
NKP = 4112
NKS = 2096
EPS = 1e-6
SCALE = 192 ** -0.5
TS = 32


class Buf:
    __slots__ = ("w", "r", "excl")

    def __init__(self, excl=False):
        self.w = None
        self.r = []
        self.excl = excl


class Eng:
    def __init__(self, nc, name, e, step=1, pe=False):
        self.name = name
        self.e = e
        self.sem = nc.alloc_semaphore("s_" + name)
        self.n = 0
        self.step = step
        self.pe = pe
        self.seen = {}


class Sched:
    def __init__(self, nc):
        self.nc = nc
        self.pe = Eng(nc, "pe", nc.tensor, pe=True)
        self.act = Eng(nc, "act", nc.scalar)
        self.dve = Eng(nc, "dve", nc.vector)
        self.pool = Eng(nc, "pool", nc.gpsimd)
        self.sp = Eng(nc, "sp", nc.sync)
        self.d_sp = [Eng(nc, f"dsp{i}", None, step=16) for i in range(24)]
        self.d_pool = [Eng(nc, f"dpl{i}", None, step=16) for i in range(8)]
        self.qi = {"sp": 0, "pool": 0}
        self.real = [self.pe, self.act, self.dve, self.pool, self.sp]
        self.all = self.real + self.d_sp + self.d_pool

    def _wait(self, eng, reads, writes):
        best = {}
        for b in reads:
            if b.w is not None:
                e2, seq = b.w
                if best.get(e2, 0) < seq:
                    best[e2] = seq
            if b.excl:
                for (e2, seq) in b.r:
                    if e2 is not eng and best.get(e2, 0) < seq:
                        best[e2] = seq
        for b in writes:
            if b.w is not None:
                e2, seq = b.w
                if best.get(e2, 0) < seq:
                    best[e2] = seq
            for (e2, seq) in b.r:
                if best.get(e2, 0) < seq:
                    best[e2] = seq
        for e2, seq in best.items():
            if e2 is eng and eng.pe:
                continue
            if eng.seen.get(e2, 0) >= seq:
                continue
            eng.e.wait_ge(e2.sem, seq * e2.step)
            eng.seen[e2] = seq

    def _mark(self, tag, reads, writes):
        for b in reads:
            if len(b.r) > 6:
                m = {}
                for (e2, s) in b.r:
                    if m.get(e2, 0) < s:
                        m[e2] = s
                b.r = list(m.items())
            b.r.append(tag)
        for b in writes:
            b.w = tag
            b.r = []

    dead = False
    opc = 0
    limit = 10 ** 9

    trace = []

    def op(self, eng, fn, reads=(), writes=(), inc=True):
        Sched.opc += 1
        if Sched.trace is not None:
            import sys as _s
            f_ = _s._getframe(1); ln = []
            while f_ is not None and len(ln) < 4:
                ln.append(f_.f_lineno); f_ = f_.f_back
            Sched.trace.append((Sched.opc, eng.name, ln))
        if Sched.opc > Sched.limit:
            Sched.dead = True
        if Sched.dead:
            if eng.pe and not inc:
                return None
            if eng.pe:
                return None
            return None
        self._wait(eng, reads, writes)
        inst = fn(eng.e)
        if inc:
            inst.then_inc(eng.sem, 1)
            eng.n += 1
            tag = (eng, eng.n)
        else:
            tag = (eng, eng.n + 1)
        self._mark(tag, reads, writes)
        return inst

    def dma(self, q, out, in_, reads=(), writes=()):
        Sched.opc += 1
        if Sched.opc > Sched.limit:
            Sched.dead = True
        if Sched.dead:
            return None
        eng, ring = (self.sp, self.d_sp) if q == "sp" else (self.pool, self.d_pool)
        d = ring[self.qi[q] % len(ring)]
        self.qi[q] += 1
        self._wait(eng, reads, writes)
        if d.n > 0 and eng.seen.get(d, 0) < d.n:
            eng.e.wait_ge(d.sem, d.n * 16)
            eng.seen[d] = d.n
        inst = eng.e.dma_start(out=out, in_=in_)
        inst.then_inc(d.sem, 16)
        d.n += 1
        self._mark((d, d.n), reads, writes)
        return inst

    def barrier(self):
        for e in self.real:
            for o in self.all:
                if o is e or o.n == 0:
                    continue
                if e.seen.get(o, 0) >= o.n:
                    continue
                e.e.wait_ge(o.sem, o.n * o.step)
                e.seen[o] = o.n


class T:
    def __init__(self, ap):
        self.a = ap
        self.b = Buf()


def build_program():
    import os
    nc = bass.Bass("TRN2", target_bir_lowering=False)
    S = Sched(nc)
    PE, ACT, DVE, POOL = S.pe, S.act, S.dve, S.pool

    def din(name, shape):
        return nc.dram_tensor(name, list(shape), F32, kind="ExternalInput").ap()

    def dout(name, shape):
        return T(nc.dram_tensor(name, list(shape), F32, kind="ExternalOutput").ap())

    x_own = din("x_own", (NOWN, D)); x_ctx = din("x_ctx", (NOWN, D))
    x_meta = din("x_meta", (16, D)); x_smp = din("x_smp", (64, D))
    cl = din("cl", (2, 2048, 256)); cml = din("cml", (2, 16, 256))
    ck = din("ck", (2, 2048, 64)); cmk = din("cmk", (2, 16, 64))
    st_re = din("st_re", (2, 64, 64)); st_im = din("st_im", (2, 64, 64))
    rope_own = din("rope_own", (NOWN, 64)); rope_ctx = din("rope_ctx", (NOWN, 64))
    rope_meta = din("rope_meta", (16, 64)); rope_smp = din("rope_smp", (64, 64))
    ctx_bias = din("ctx_bias", (128, 1)); flagb = din("flagb", (128, 32))
    mask_b = din("mask_b", (128, 2)); mask_c = din("mask_c", (128, 32)); chunk_mask = din("chunk_mask", (128, 128))
    norm_mix = din("norm_mix", (1, D)); w_in = din("w_in", (D, 3776))
    q_lora_norm = din("q_lora_norm", (1, 384)); w_uq = din("w_uq", (384, 1536))
    q_nope_norm = din("q_nope_norm", (1, 128)); q_rope_norm = din("q_rope_norm", (1, 64))
    kv_lora_norm = din("kv_lora_norm", (1, 256)); k_rope_norm = din("k_rope_norm", (1, 64))
    w_ukv = din("w_ukv", (256, 2048)); k_nope_norm = din("k_nope_norm", (1, 128))
    w_o = din("w_o_attn", (1024, D))
    a_re = din("ssm_a_re", (64, 64)); a_im = din("ssm_a_im", (64, 64)); log_dt = din("ssm_log_dt", (64,))
    b_re = din("ssm_b_re", (64, 64, 16)); b_im = din("ssm_b_im", (64, 64, 16))
    c_re = din("ssm_c_re", (64, 16, 64)); c_im = din("ssm_c_im", (64, 16, 64))
    ssm_d = din("ssm_d", (D,))
    w_glu_v = din("w_glu_v", (D, D)); w_glu_g = din("w_glu_g", (D, D)); w_out = din("w_out", (D, D))
    norm_mlp = din("norm_mlp", (1, D)); w_up = din("w_mlp_up", (D, 4096)); w_down = din("w_mlp_down", (4096, D))

    y_own = dout("y_own", (NOWN, D)); y_smp = dout("y_smp", (64, D))
    lat_own = dout("lat_own", (NOWN, 256)); kr_own = dout("kr_own", (NOWN, 64))
    lat_meta = dout("lat_meta", (16, 256)); kr_meta = dout("kr_meta", (16, 64))
    so_re = dout("so_re", (64, 64)); so_im = dout("so_im", (64, 64))
    lat_smp = dout("lat_smp", (64, 256)); kr_smp = dout("kr_smp", (64, 64))
    ss_re = dout("ss_re", (2, 64, 64)); ss_im = dout("ss_im", (2, 64, 64))
    outs = [y_own, y_smp, lat_own, kr_own, lat_meta, kr_meta, so_re, so_im, lat_smp, kr_smp, ss_re, ss_im]

    SK = os.environ.get("KSCR", "ExternalOutput")
    Qs = T(nc.dram_tensor("Qs", [128, 12, NF], BF16, kind=SK).ap())
    Ss = T(nc.dram_tensor("Ss", [128, 8, NF], BF16, kind=SK).ap())
    Os = T(nc.dram_tensor("Os", [128, 8, NF], BF16, kind=SK).ap())
    H1 = T(nc.dram_tensor("H1", [NF, D], F32, kind=SK).ap())
    Us = T(nc.dram_tensor("Us", [128, 8, 32 + 4096 + 64], BF16, kind=SK).ap())

    psA = nc.alloc_psum_tensor("psA", [128, 6, 512], F32).ap()
    psT = nc.alloc_psum_tensor("psT", [128, 2, 1024], BF16).ap()
    psX = psT[:, 1, :].bitcast(F32)
    pb = [Buf(excl=True) for _ in range(6)]
    ptb = [Buf(excl=True) for _ in range(2)]
    pxb = ptb[1]

    def bcast_row(src, n):
        return bass.AP(src.tensor, 0, [[0, 128], [1, n]])

    def mm(out, lhsT, rhs, start, stop, R, W, inc, **kw):
        S.op(PE, lambda e: e.matmul(out, lhsT, rhs, start=start, stop=stop, **kw), reads=R, writes=W, inc=inc)

    def tt(eng, out, a, b, op, R, W):
        S.op(eng, lambda e: e.tensor_tensor(out=out, in0=a, in1=b, op=op), reads=R, writes=W)

    def ts(eng, out, a, s1, s2, op0, op1, R, W):
        S.op(eng, lambda e: e.tensor_scalar(out=out, in0=a, scalar1=s1, scalar2=s2, op0=op0, op1=op1), reads=R, writes=W)

    def ts1(eng, out, a, s1, op, R, W):
        S.op(eng, lambda e: e.tensor_single_scalar(out=out, in_=a, scalar=s1, op=op), reads=R, writes=W)

    def act(out, in_, func, R, W, **kw):
        if "accum_out" in kw:
            ao = kw["accum_out"]
            S.op(DVE, lambda e: e.memset(ao, 0.0), writes=[W[-1]])
        S.op(ACT, lambda e: e.activation(out=out, in_=in_, func=func, **kw), reads=R, writes=W)

    def cp(eng, out, in_, R, W):
        if eng is ACT:
            S.op(eng, lambda e: e.activation(out=out, in_=in_, func=AF.Copy), reads=R, writes=W)
        else:
            S.op(eng, lambda e: e.tensor_copy(out=out, in_=in_), reads=R, writes=W)

    def bc(ap, shape, axis):
        return ap.unsqueeze(axis).broadcast_to(list(shape))

    top = ExitStack()
    KSTOP = int(os.environ.get("KSTOP", "9"))
    Sched.limit = int(os.environ.get("KLIMIT", str(10 ** 9)))
    Sched.opc = 0
    Sched.dead = False

    def finish_now():
        Sched.dead = False
        Sched.limit = 10 ** 9
        for _ in range(int(os.environ.get("KPAD", "0"))):
            if os.environ.get("KPADE", "dve") == "dve":
                S.op(DVE, lambda e: e.memset(halfpi.a, 1.5), writes=[halfpi.b])
            else:
                S.op(ACT, lambda e: e.activation(out=halfpi.a, in_=halfpi.a, func=AF.Copy), reads=[halfpi.b], writes=[halfpi.b])
        S._wait(S.sp, [o.b for o in outs], [o.b for o in outs])
        S.barrier()
        return nc

    def alloc(stack, name, shape, dt=F32):
        return T(stack.enter_context(nc.sbuf_tensor(name, list(shape), dt)).ap())

    ident = alloc(top, "ident", [128, 128], BF16)
    identf = alloc(top, "identf", [128, 128], F32)
    ones = alloc(top, "ones", [128, 128], BF16)
    halfpi = alloc(top, "halfpi", [128, 1])
    for t_ in (ident, identf):
        S.op(POOL, lambda e: e.memset(t_.a, 1.0), writes=[t_.b])
        S.op(POOL, lambda e: e.affine_select(out=t_.a, in_=t_.a, pattern=[[-1, 128]], compare_op=ALU.is_equal,
                                             fill=0.0, base=0, channel_multiplier=1), reads=[t_.b], writes=[t_.b])
    S.op(POOL, lambda e: e.memset(ones.a, 1.0), writes=[ones.b])
    S.op(POOL, lambda e: e.memset(halfpi.a, math.pi / 2), writes=[halfpi.b])
    rs_ring = [alloc(top, f"rs{i}", [128, 16]) for i in range(4)]
    rs_i = [0]

    def rstd_of(ss_ap, ssb, n, inv_d, nt):
        r = rs_ring[rs_i[0] % 4]; rs_i[0] += 1
        ts(DVE, r.a[:nt, 0:n], ss_ap, inv_d, EPS, ALU.mult, ALU.add, [ssb], [r.b])
        act(r.a[:nt, 0:n], r.a[:nt, 0:n], AF.Sqrt, [r.b], [r.b])
        S.op(DVE, lambda e: e.reciprocal(out=r.a[:nt, 0:n], in_=r.a[:nt, 0:n]), reads=[r.b], writes=[r.b])
        return r

    pkv = ExitStack()
    p1 = ExitStack()
    KVp = alloc(pkv, "KVp", [128, 3, NKP], BF16)
    KVso = alloc(pkv, "KVso", [128, 3, 64], BF16)

    if KSTOP == -1:
        return finish_now()
    NP = 32
    def pl(name): return alloc(p1, name, [128, NP])
    are = pl("are"); aim = pl("aim"); dtp = pl("dtp"); mag = pl("mag"); lr = pl("lr"); li = pl("li")
    fr = pl("fr"); fi = pl("fi"); t0 = pl("t0"); t1 = pl("t1"); t2 = pl("t2"); t3 = pl("t3")
    with nc.allow_non_contiguous_dma("small strided param loads"):
        for g2 in range(2):
            sl = slice(64 * g2, 64 * g2 + 64)
            S.dma("sp", are.a[sl, :], a_re.rearrange("(p g) n -> g n p", g=2)[g2], writes=[are.b])
            S.dma("sp", aim.a[sl, :], a_im.rearrange("(p g) n -> g n p", g=2)[g2], writes=[aim.b])
            S.dma("sp", dtp.a[sl, :], bass.AP(log_dt.tensor, g2, [[0, 64], [2, 32]]), writes=[dtp.b])
    act(dtp.a, dtp.a, AF.Exp, [dtp.b], [dtp.b])
    tt(DVE, t0.a, are.a, dtp.a, ALU.mult, [are.b, dtp.b], [t0.b])
    tt(DVE, t1.a, aim.a, dtp.a, ALU.mult, [aim.b, dtp.b], [t1.b])
    act(mag.a, t0.a, AF.Exp, [t0.b], [mag.b])
    c1 = pl("c1"); s1 = pl("s1")
    act(s1.a, t1.a, AF.Sin, [t1.b], [s1.b], scale=0.125)
    act(c1.a, t1.a, AF.Sin, [t1.b, halfpi.b], [c1.b], scale=-0.125, bias=halfpi.a)

    def csq(cr, ci):
        tt(DVE, t2.a, cr.a, cr.a, ALU.mult, [cr.b], [t2.b])
        tt(DVE, t3.a, ci.a, ci.a, ALU.mult, [ci.b], [t3.b])
        tt(DVE, ci.a, cr.a, ci.a, ALU.mult, [cr.b, ci.b], [ci.b])
        ts1(DVE, ci.a, ci.a, 2.0, ALU.mult, [ci.b], [ci.b])
        tt(DVE, cr.a, t2.a, t3.a, ALU.subtract, [t2.b, t3.b], [cr.b])
    for _ in range(3):
        csq(c1, s1)
    tt(DVE, lr.a, mag.a, c1.a, ALU.mult, [mag.b, c1.b], [lr.b])
    tt(DVE, li.a, mag.a, s1.a, ALU.mult, [mag.b, s1.b], [li.b])

    def cmul(outr, outi, ar_, ai_, br_, bi_, ta, tb):
        tt(DVE, ta[0], ar_[0], br_[0], ALU.mult, [ar_[1], br_[1]], [ta[1]])
        tt(DVE, tb[0], ai_[0], bi_[0], ALU.mult, [ai_[1], bi_[1]], [tb[1]])
        tt(DVE, outr[0], ta[0], tb[0], ALU.subtract, [ta[1], tb[1]], [outr[1]])
        tt(DVE, ta[0], ar_[0], bi_[0], ALU.mult, [ar_[1], bi_[1]], [ta[1]])
        tt(DVE, tb[0], ai_[0], br_[0], ALU.mult, [ai_[1], br_[1]], [tb[1]])
        tt(DVE, outi[0], ta[0], tb[0], ALU.add, [ta[1], tb[1]], [outi[1]])

    P = lambda t_: (t_.a, t_.b)
    den = pl("den"); lm1 = pl("lm1"); nai = pl("nai")
    tt(DVE, t2.a, are.a, are.a, ALU.mult, [are.b], [t2.b])
    tt(DVE, t3.a, aim.a, aim.a, ALU.mult, [aim.b], [t3.b])
    tt(DVE, den.a, t2.a, t3.a, ALU.add, [t2.b, t3.b], [den.b])
    S.op(DVE, lambda e: e.reciprocal(out=den.a, in_=den.a), reads=[den.b], writes=[den.b])
    ts1(DVE, lm1.a, lr.a, -1.0, ALU.add, [lr.b], [lm1.b])
    ts1(DVE, nai.a, aim.a, -1.0, ALU.mult, [aim.b], [nai.b])
    cmul(P(fr), P(fi), P(lm1), P(li), P(are), P(nai), P(t2), P(t3))
    tt(DVE, fr.a, fr.a, den.a, ALU.mult, [fr.b, den.b], [fr.b])
    tt(DVE, fi.a, fi.a, den.a, ALU.mult, [fi.b, den.b], [fi.b])
    Er = alloc(p1, "Er", [128, NP, TS]); Ei = alloc(p1, "Ei", [128, NP, TS]); dec0 = alloc(p1, "dec0", [128, NP, TS])
    tw0 = alloc(p1, "tw0", [128, NP, TS // 2]); tw1 = alloc(p1, "tw1", [128, NP, TS // 2])
    S.op(DVE, lambda e: e.memset(Er.a[:, :, 0:1], 1.0), writes=[Er.b])
    S.op(DVE, lambda e: e.memset(Ei.a[:, :, 0:1], 0.0), writes=[Ei.b])
    pr = pl("pr"); pi_ = pl("pi")
    cp(DVE, pr.a, c1.a, [c1.b], [pr.b]); cp(DVE, pi_.a, s1.a, [s1.b], [pi_.b])
    w = 1
    while w < TS:
        prb = bc(pr.a, [128, NP, w], 2); pib = bc(pi_.a, [128, NP, w], 2)
        a0 = Er.a[:, :, 0:w]; b0 = Ei.a[:, :, 0:w]
        ta = tw0.a[:, :, 0:w]; tb = tw1.a[:, :, 0:w]
        tt(DVE, ta, a0, prb, ALU.mult, [Er.b, pr.b], [tw0.b])
        tt(DVE, tb, b0, pib, ALU.mult, [Ei.b, pi_.b], [tw1.b])
        tt(DVE, Er.a[:, :, w:2 * w], ta, tb, ALU.subtract, [tw0.b, tw1.b], [Er.b])
        tt(DVE, ta, a0, pib, ALU.mult, [Er.b, pi_.b], [tw0.b])
        tt(DVE, tb, b0, prb, ALU.mult, [Ei.b, pr.b], [tw1.b])
        tt(DVE, Ei.a[:, :, w:2 * w], ta, tb, ALU.add, [tw0.b, tw1.b], [Ei.b])
        csq(pr, pi_)
        w *= 2
    cp(DVE, dec0.a, bc(mag.a, [128, NP, TS], 2), [mag.b], [dec0.b])
    S.op(DVE, lambda e: e.memset(dec0.a[:, :, 0:1], 0.0), writes=[dec0.b])
    LamT = {}; FT = {}
    for tv in (16, 32):
        Lr_ = pl(f"Lr{tv}"); Li_ = pl(f"Li{tv}"); Fr_ = pl(f"Fr{tv}"); Fi_ = pl(f"Fi{tv}")
        er = (Er.a[:, :, tv - 1], Er.b); ei = (Ei.a[:, :, tv - 1], Ei.b)
        cmul(P(Lr_), P(Li_), P(lr), P(li), er, ei, P(t2), P(t3))
        cmul(P(Fr_), P(Fi_), P(fr), P(fi), er, ei, P(t2), P(t3))
        LamT[tv] = (Lr_, Li_); FT[tv] = (Fr_, Fi_)
    LFr = pl("LFr"); LFi = pl("LFi"); nfi = pl("nfi")
    tt(DVE, t2.a, fr.a, fr.a, ALU.mult, [fr.b], [t2.b])
    tt(DVE, t3.a, fi.a, fi.a, ALU.mult, [fi.b], [t3.b])
    tt(DVE, den.a, t2.a, t3.a, ALU.add, [t2.b, t3.b], [den.b])
    S.op(DVE, lambda e: e.reciprocal(out=den.a, in_=den.a), reads=[den.b], writes=[den.b])
    ts1(DVE, nfi.a, fi.a, -1.0, ALU.mult, [fi.b], [nfi.b])
    cmul(P(LFr), P(LFi), P(lr), P(li), P(fr), P(nfi), P(t2), P(t3))
    tt(DVE, LFr.a, LFr.a, den.a, ALU.mult, [LFr.b, den.b], [LFr.b])
    tt(DVE, LFi.a, LFi.a, den.a, ALU.mult, [LFi.b, den.b], [LFi.b])
    flg = pl("flg"); S.dma("sp", flg.a, flagb, writes=[flg.b])

    if KSTOP == -2:
        return finish_now()
    BT = alloc(p1, "BT", [128, 8, 2, 128], BF16)
    CT = alloc(p1, "CT", [128, 8, 4, 2, 128], BF16)
    Dd = alloc(p1, "Dd", [128, 8, 128], BF16)
    with ExitStack() as su:
        Bn = [alloc(su, f"Bn{i}", [64, 64, 16]) for i in range(2)]
        Cn = [alloc(su, f"Cn{i}", [16, 64, 64]) for i in range(2)]
        Cl = [alloc(su, f"Cl{i}", [128, 32, 16]) for i in range(2)]
        Cp = [alloc(su, f"Cp{i}", [128, 32, 16]) for i in range(2)]
        ctmp = [alloc(su, f"ctmp{i}", [128, 32, 16]) for i in range(2)]
        mB = alloc(su, "mB", [128, 2]); mC = alloc(su, "mC", [128, 4, 8]); dcol = alloc(su, "dcol", [128, 8])
        S.dma("sp", mB.a, mask_b, writes=[mB.b])
        S.dma("sp", mC.a, mask_c.rearrange("p (a b) -> p a b", a=4), writes=[mC.b])
        with nc.allow_non_contiguous_dma("one-time ssm weight relayout"):
            S.dma("sp", dcol.a, ssm_d.rearrange("(t p) -> p t", p=128), writes=[dcol.b])
        for i, src in enumerate((b_re, b_im)):
            S.dma("sp", Bn[i].a, src.rearrange("g n c -> n g c"), writes=[Bn[i].b])
        for i, src in enumerate((c_re, c_im)):
            S.dma("sp", Cn[i].a, src.rearrange("g c n -> c g n"), writes=[Cn[i].b])
        for i in range(2):
            bp = psA[:, i, :].rearrange("p (t n) -> p t n", t=8)
            for t_ in range(8):
                S.op(PE, lambda e: e.transpose(bp[:, t_, :], Bn[i].a[:, 8 * t_:8 * t_ + 8, :].rearrange("p a b -> p (a b)"), identf.a[:64, :64]),
                     reads=[Bn[i].b, identf.b], writes=[pb[i]], inc=(t_ == 7))
            tt(DVE, BT.a[:, :, i, :].rearrange("p t (g n) -> p t g n", g=2), bc(bp, [128, 8, 2, 64], 2),
               mB.a.unsqueeze(1).unsqueeze(3).broadcast_to([128, 8, 2, 64]), ALU.mult, [pb[i], mB.b], [BT.b])
            cpp = psA[:, 2 + i, :].rearrange("p (t c) -> p t c", t=32)
            for tp_ in range(32):
                S.op(PE, lambda e: e.transpose(cpp[:, tp_, :], Cn[i].a[:, 2 * tp_:2 * tp_ + 2, :].rearrange("p a b -> p (a b)"), identf.a[:16, :16]),
                     reads=[Cn[i].b, identf.b], writes=[pb[2 + i]], inc=(tp_ == 31))
            cp(ACT, Cl[i].a, cpp, [pb[2 + i]], [Cl[i].b])
        frb = bc(fr.a, [128, 32, 16], 2); fib = bc(fi.a, [128, 32, 16], 2)
        tt(DVE, ctmp[0].a, Cl[0].a, frb, ALU.mult, [Cl[0].b, fr.b], [ctmp[0].b])
        tt(DVE, ctmp[1].a, Cl[1].a, fib, ALU.mult, [Cl[1].b, fi.b], [ctmp[1].b])
        tt(DVE, Cp[0].a, ctmp[0].a, ctmp[1].a, ALU.subtract, [ctmp[0].b, ctmp[1].b], [Cp[0].b])
        tt(DVE, ctmp[0].a, Cl[0].a, fib, ALU.mult, [Cl[0].b, fi.b], [ctmp[0].b])
        tt(DVE, ctmp[1].a, Cl[1].a, frb, ALU.mult, [Cl[1].b, fr.b], [ctmp[1].b])
        tt(DVE, Cp[1].a, ctmp[0].a, ctmp[1].a, ALU.add, [ctmp[0].b, ctmp[1].b], [Cp[1].b])
        ts1(DVE, Cp[1].a, Cp[1].a, -1.0, ALU.mult, [Cp[1].b], [Cp[1].b])
        for tl in range(8):
            for pln in range(2):
                tt(DVE, CT.a[:, tl, :, pln, :].rearrange("p a (g c) -> p a g c", g=8),
                   bc(Cp[pln].a[:, 4 * tl:4 * tl + 4, :], [128, 4, 8, 16], 2),
                   bc(mC.a, [128, 4, 8, 16], 3), ALU.mult, [Cp[pln].b, mC.b], [CT.b])
            ts1(DVE, Dd.a[:, tl, :], identf.a, dcol.a[:, tl:tl + 1], ALU.mult, [identf.b, dcol.b], [Dd.b])
        S.barrier()

    if KSTOP == 0:
        return finish_now()
    injr = pl("injr"); inji = pl("inji"); injmr = pl("injmr"); injmi = pl("injmi")
    hor = pl("hor"); hoi = pl("hoi"); h0r = pl("h0r"); h0i = pl("h0i")
    f2 = lambda ap: ap.rearrange("p a b -> p (a b)")
    from types import SimpleNamespace as NS
    NLANE = int(os.environ.get("KLANES", "2"))

    def run_lanes(factories, lanes, stagger):
        slots = [None] * len(lanes); delay = [stagger * i for i in range(len(lanes))]; idx = 0
        while True:
            busy = False
            for li in range(len(lanes)):
                if slots[li] is None and idx < len(factories):
                    if delay[li] > 0:
                        delay[li] -= 1; busy = True
                        continue
                    slots[li] = factories[idx](lanes[li]); idx += 1
                if slots[li] is not None:
                    busy = True
                    try:
                        next(slots[li])
                    except StopIteration:
                        slots[li] = None
            if not busy:
                break

    pw = ExitStack()
    g_mix = alloc(pw, "g_mix", [128, D]); g_q = alloc(pw, "g_q", [128, 384]); g_kv = alloc(pw, "g_kv", [128, 256])
    g_kr = alloc(pw, "g_kr", [128, 64]); g_qn = alloc(pw, "g_qn", [128, 128]); g_qr = alloc(pw, "g_qr", [128, 64])
    for t_, src, n in ((g_mix, norm_mix, D), (g_q, q_lora_norm, 384), (g_kv, kv_lora_norm, 256), (g_kr, k_rope_norm, 64),
                       (g_qn, q_nope_norm, 128), (g_qr, q_rope_norm, 64)):
        S.dma("sp", t_.a, bcast_row(src, n), writes=[t_.b])
    w_zs = alloc(pw, "w_zs", [128, 8, 704], BF16); w_u = alloc(pw, "w_u", [128, 8, 1024], BF16)
    w_q = alloc(pw, "w_q", [128, 3, 1536], BF16)
    w_in_v = w_in.rearrange("(k p) c -> p k c", p=128)
    S.dma("pool", w_zs.a, w_in_v[:, :, 0:704], writes=[w_zs.b])
    S.dma("pool", w_u.a, w_in_v[:, :, 704:1728], writes=[w_u.b])
    S.dma("pool", w_q.a, w_uq.rearrange("(k p) c -> p k c", p=128), writes=[w_q.b])
    pa = ExitStack()

    def mk_lane_a(li):
        A_ = lambda n, shp, dt=F32: alloc(pa, f"{n}_{li}", shp, dt)
        return NS(xt=A_("xt", [128, D]), junk=A_("junk", [128, 1536], BF16), ssq=A_("ssq", [128, 4]), xs=A_("xs", [128, D], BF16),
                  xnT=A_("xnT", [128, 8, 128], BF16), cq=A_("cq", [128, 384], BF16), ckv=A_("ckv", [128, 256]),
                  ckvb=A_("ckvb", [128, 256], BF16), krn=A_("krn", [128, 64]), kro=A_("kro", [128, 64]), krd=A_("krd", [128, 128], BF16),
                  rt=[A_(f"rt{i}", [128, 8, 32]) for i in range(4)], rope_t=A_("rope_t", [128, 64]), cqT=A_("cqT", [128, 3, 128], BF16),
                  sqq=A_("sqq", [128, 1536]), ssh=A_("ssh", [128, 16]), qtmp=A_("qtmp", [128, 8, 128]), qr=A_("qr", [128, 8, 64]),
                  Qn=A_("Qn", [128, 8, 128], BF16), Qrp=A_("Qrp", [128, 8, 64], BF16), QT=A_("QT", [128, 12, 128], BF16),
                  uT=A_("uT", [128, 8, 128], BF16))
    lanes_a = [mk_lane_a(li) for li in range(int(os.environ.get("KLA", NLANE)))]

    def rope_apply(L, dst_lo, dst_hi, src_lo, src_hi, cosb, sinb, R, W, shp):
        rt = L.rt
        a, b_, c_, d_ = [rt[i].a[:shp[0], :shp[1], :] if len(shp) == 3 else rt[i].a[:shp[0], 0, :] for i in range(4)]
        bs = [rt[i].b for i in range(4)]
        tt(DVE, a, src_lo, cosb, ALU.mult, R, [bs[0]])
        tt(DVE, b_, src_hi, sinb, ALU.mult, R, [bs[1]])
        tt(DVE, dst_lo, a, b_, ALU.subtract, [bs[0], bs[1]], W)
        tt(DVE, c_, src_hi, cosb, ALU.mult, R, [bs[2]])
        tt(DVE, d_, src_lo, sinb, ALU.mult, R, [bs[3]])
        tt(DVE, dst_hi, c_, d_, ALU.add, [bs[2], bs[3]], W)

    def token_block(L, kind, nt, src, rope_src, kvdst, lat_dst, kr_dst, fo, ucol, ucols):
        x = L.xt; sq_ = L.ssq; rp = L.rope_t; u = L.uT; ck_ = L.ckv; ko = L.kro; junk = L.junk; xs = L.xs; xnT = L.xnT
        cq = L.cq; ckvb = L.ckvb; krn = L.krn; krd = L.krd; cqT = L.cqT; sqq = L.sqq; ssh = L.ssh; qtmp = L.qtmp; qr = L.qr
        Qn = L.Qn; Qrp = L.Qrp; QT = L.QT
        S.dma("sp", x.a[:nt, :], src, writes=[x.b])
        S.dma("sp", rp.a[:nt, :], rope_src, writes=[rp.b])
        yield
        act(junk.a[:nt, 0:D], x.a[:nt, :], AF.Square, [x.b], [junk.b, sq_.b], accum_out=sq_.a[:nt, 0:1])
        r = rstd_of(sq_.a[:nt, 0:1], sq_.b, 1, 1.0 / D, nt)
        S.op(DVE, lambda e: e.scalar_tensor_tensor(out=xs.a[:nt, :], in0=x.a[:nt, :], scalar=r.a[:nt, 0:1], in1=g_mix.a[:nt, :],
                                                   op0=ALU.mult, op1=ALU.mult), reads=[x.b, r.b, g_mix.b], writes=[xs.b])
        yield
        pt = psT[:, 0, :].rearrange("p (k t) -> p k t", k=8)
        for k in range(8):
            S.op(PE, lambda e: e.transpose(pt[:, k, :nt], xs.a[:nt, 128 * k:128 * k + 128], ident.a[:nt, :nt]),
                 reads=[xs.b, ident.b], writes=[ptb[0]], inc=(k == 7))
        cp(ACT, xnT.a[:, :, :nt], pt[:, :, :nt], [ptb[0]], [xnT.b])
        yield
        zlo = 0 if kind != "ctx" else 384
        zs = psA[:, 0:2, :].rearrange("p a b -> p (a b)")
        for (ca, cb) in ((0, 512), (512, 704)):
            if cb <= zlo:
                continue
            ca2 = max(ca, zlo)
            for k in range(8):
                mm(zs[:nt, ca2:cb], xnT.a[:, k, :nt], w_zs.a[:, k, ca2:cb], k == 0, k == 7, [xnT.b, w_zs.b],
                   [pb[ca // 512]], inc=(k == 7))
        act(junk.a[:nt, 0:256], zs[:nt, 384:640], AF.Square, [pb[0], pb[1]], [junk.b, sq_.b], accum_out=sq_.a[:nt, 1:2])
        r = rstd_of(sq_.a[:nt, 1:2], sq_.b, 1, 1.0 / 256, nt)
        S.op(DVE, lambda e: e.scalar_tensor_tensor(out=ck_.a[:nt, :], in0=zs[:nt, 384:640], scalar=r.a[:nt, 0:1], in1=g_kv.a[:nt, :],
                                                   op0=ALU.mult, op1=ALU.mult), reads=[pb[0], pb[1], r.b, g_kv.b], writes=[ck_.b])
        cp(ACT, ckvb.a[:nt, :], ck_.a[:nt, :], [ck_.b], [ckvb.b])
        if lat_dst is not None:
            S.dma("sp", lat_dst[0].a[lat_dst[1]:lat_dst[1] + nt, :], ck_.a[:nt, :], reads=[ck_.b], writes=[lat_dst[0].b])
        act(junk.a[:nt, 0:64], zs[:nt, 640:704], AF.Square, [pb[1]], [junk.b, sq_.b], accum_out=sq_.a[:nt, 2:3])
        r = rstd_of(sq_.a[:nt, 2:3], sq_.b, 1, 1.0 / 64, nt)
        S.op(DVE, lambda e: e.scalar_tensor_tensor(out=krn.a[:nt, :], in0=zs[:nt, 640:704], scalar=r.a[:nt, 0:1], in1=g_kr.a[:nt, :],
                                                   op0=ALU.mult, op1=ALU.mult), reads=[pb[1], r.b, g_kr.b], writes=[krn.b])
        if kind != "ctx":
            act(junk.a[:nt, 0:384], zs[:nt, 0:384], AF.Square, [pb[0]], [junk.b, sq_.b], accum_out=sq_.a[:nt, 3:4])
            r = rstd_of(sq_.a[:nt, 3:4], sq_.b, 1, 1.0 / 384, nt)
            S.op(DVE, lambda e: e.scalar_tensor_tensor(out=cq.a[:nt, :], in0=zs[:nt, 0:384], scalar=r.a[:nt, 0:1], in1=g_q.a[:nt, :],
                                                       op0=ALU.mult, op1=ALU.mult), reads=[pb[0], r.b, g_q.b], writes=[cq.b])
        yield
        rope_apply(L, ko.a[:nt, 0:32], ko.a[:nt, 32:64], krn.a[:nt, 0:32], krn.a[:nt, 32:64], rp.a[:nt, 0:32], rp.a[:nt, 32:64],
                   [krn.b, rp.b], [ko.b], (nt, 32))
        if kr_dst is not None:
            S.dma("sp", kr_dst[0].a[kr_dst[1]:kr_dst[1] + nt, :], ko.a[:nt, :], reads=[ko.b], writes=[kr_dst[0].b])
        cp(ACT, krd.a[:nt, 0:64], ko.a[:nt, :], [ko.b], [krd.b])
        cp(ACT, krd.a[:nt, 64:128], ko.a[:nt, :], [ko.b], [krd.b])
        yield
        pt2 = psT[:, 1, :].rearrange("p (k t) -> p k t", k=8)
        S.op(PE, lambda e: e.transpose(pt2[:, 0, :nt], ckvb.a[:nt, 0:128], ident.a[:nt, :nt]), reads=[ckvb.b, ident.b], writes=[ptb[1]], inc=False)
        S.op(PE, lambda e: e.transpose(pt2[:, 1, :nt], ckvb.a[:nt, 128:256], ident.a[:nt, :nt]), reads=[ckvb.b, ident.b], writes=[ptb[1]], inc=False)
        S.op(PE, lambda e: e.transpose(pt2[:, 2, :nt], krd.a[:nt, :], ident.a[:nt, :nt]), reads=[krd.b, ident.b], writes=[ptb[1]],
             inc=(kind == "ctx"))
        if kind != "ctx":
            for k in range(3):
                S.op(PE, lambda e: e.transpose(pt2[:, 3 + k, :nt], cq.a[:nt, 128 * k:128 * k + 128], ident.a[:nt, :nt]),
                     reads=[cq.b, ident.b], writes=[ptb[1]], inc=(k == 2))
        for (kt, kc, ta_, tb_) in kvdst:
            cp(ACT, kt.a[:, :, kc:kc + (tb_ - ta_)], pt2[:, 0:3, ta_:tb_], [ptb[1]], [kt.b])
        if kind != "ctx":
            cp(ACT, cqT.a[:, :, :nt], pt2[:, 3:6, :nt], [ptb[1]], [cqT.b])
        yield
        if nt < 128:
            S.op(POOL, lambda e: e.memset(u.a, 0.0), writes=[u.b])
        up = psA[:, 4:6, :].rearrange("p a (m t) -> p (a m) t", m=4)
        for m in range(8):
            for k in range(8):
                mm(up[:, m, :nt], w_u.a[:, k, 128 * m:128 * m + 128], xnT.a[:, k, :nt], k == 0, k == 7, [w_u.b, xnT.b],
                   [pb[4 + m // 4]], inc=(k == 7))
        cp(ACT, u.a[:, :, :nt], up[:, :, :nt], [pb[4], pb[5]], [u.b])
        S.dma("sp", Us.a[:, :, ucol:ucol + ucols], u.a[:, :, :ucols], reads=[u.b], writes=[Us.b])
        yield
        if kind != "ctx":
            qp = psA[:, 1:4, :].rearrange("p a b -> p (a b)")
            for n in range(3):
                for k in range(3):
                    mm(qp[:nt, 512 * n:512 * n + 512], cqT.a[:, k, :nt], w_q.a[:, k, 512 * n:512 * n + 512], k == 0, k == 2,
                       [cqT.b, w_q.b], [pb[1 + n]], inc=(k == 2))
            qb = [pb[1], pb[2], pb[3]]
            qv = qp[:nt, :].rearrange("p (h d) -> p h d", h=8)
            act(sqq.a[:nt, :], qp[:nt, :], AF.Square, qb, [sqq.b])
            sv = sqq.a[:nt, :].rearrange("p (h d) -> p h d", h=8)
            S.op(DVE, lambda e: e.tensor_reduce(out=ssh.a[:nt, 0:8], in_=sv[:, :, 0:128], axis=AX.X, op=ALU.add), reads=[sqq.b], writes=[ssh.b])
            S.op(DVE, lambda e: e.tensor_reduce(out=ssh.a[:nt, 8:16], in_=sv[:, :, 128:192], axis=AX.X, op=ALU.add), reads=[sqq.b], writes=[ssh.b])
            ts1(DVE, ssh.a[:nt, 8:16], ssh.a[:nt, 8:16], 2.0, ALU.mult, [ssh.b], [ssh.b])
            r = rstd_of(ssh.a[:nt, 0:16], ssh.b, 16, 1.0 / 128, nt)
            tt(DVE, qtmp.a[:nt], qv[:, :, 0:128], bc(r.a[:nt, 0:8], [nt, 8, 128], 2), ALU.mult, qb + [r.b], [qtmp.b])
            tt(DVE, qr.a[:nt], qv[:, :, 128:192], bc(r.a[:nt, 8:16], [nt, 8, 64], 2), ALU.mult, qb + [r.b], [qr.b])
            yield
            tt(POOL, Qn.a[:nt], qtmp.a[:nt], bc(g_qn.a[:nt, :], [nt, 8, 128], 1), ALU.mult, [qtmp.b, g_qn.b], [Qn.b])
            tt(DVE, qr.a[:nt], qr.a[:nt], bc(g_qr.a[:nt, :], [nt, 8, 64], 1), ALU.mult, [qr.b, g_qr.b], [qr.b])
            cosb = bc(rp.a[:nt, 0:32], [nt, 8, 32], 1); sinb = bc(rp.a[:nt, 32:64], [nt, 8, 32], 1)
            rope_apply(L, Qrp.a[:nt, :, 0:32], Qrp.a[:nt, :, 32:64], qr.a[:nt, :, 0:32], qr.a[:nt, :, 32:64], cosb, sinb,
                       [qr.b, rp.b], [Qrp.b], (nt, 8, 32))
            yield
            for h in range(8):
                S.op(PE, lambda e: e.transpose(pt[:, h, :nt], Qn.a[:nt, h, :], ident.a[:nt, :nt]),
                     reads=[Qn.b, ident.b], writes=[ptb[0]], inc=(h == 7))
            cp(ACT, QT.a[:, 0:8, :nt], pt[:, :, :nt], [ptb[0]], [QT.b])
            for hp in range(4):
                S.op(PE, lambda e: e.transpose(pt2[:, hp, :nt], Qrp.a[:nt, 2 * hp:2 * hp + 2, :].rearrange("p a b -> p (a b)"), ident.a[:nt, :nt]),
                     reads=[Qrp.b, ident.b], writes=[ptb[1]], inc=(hp == 3))
            cp(ACT, QT.a[:, 8:12, :nt], pt2[:, 0:4, :nt], [ptb[1]], [QT.b])
            S.dma("sp", Qs.a[:, :, fo:fo + nt], QT.a[:, :, :nt], reads=[QT.b], writes=[Qs.b])
            yield

    UC_CTX, UC_OWN, UC_SMP = 32, 32 + 2048, 32 + 4096
    fa = [lambda L: token_block(L, "meta", 16, x_meta, rope_meta, [(KVp, 4096, 0, 16)], (lat_meta, 0), (kr_meta, 0), 2048, 0, 32)]
    for bi in range(16):
        fa.append(lambda L, bi=bi: token_block(L, "ctx", 128, x_ctx[128 * bi:128 * bi + 128, :], rope_ctx[128 * bi:128 * bi + 128, :],
                                               [(KVp, 128 * bi, 0, 128)], None, None, 0, UC_CTX + 128 * bi, 128))
    for bi in range(16):
        fa.append(lambda L, bi=bi: token_block(L, "own", 128, x_own[128 * bi:128 * bi + 128, :], rope_own[128 * bi:128 * bi + 128, :],
                                               [(KVp, 2048 + 128 * bi, 0, 128)], (lat_own, 128 * bi), (kr_own, 128 * bi), 128 * bi,
                                               UC_OWN + 128 * bi, 128))
    fa.append(lambda L: token_block(L, "smp", 64, x_smp, rope_smp, [(KVso, 0, 0, 64)], (lat_smp, 0), (kr_smp, 0), 2064, UC_SMP, 64))
    run_lanes(fa, lanes_a, 6)
    S.barrier()
    pa.close()
    pw.close()

    pbk = ExitStack()

    def mk_lane_b(li):
        A_ = lambda n, shp, dt=F32: alloc(pbk, f"{n}_{li}", shp, dt)
        return NS(Xr=A_("Xr", [128, NP, TS]), Xi=A_("Xi", [128, NP, TS]), tA=A_("tA", [128, NP, TS]), tB=A_("tB", [128, NP, TS]),
                  tC=A_("tC", [128, NP, TS]), tD=A_("tD", [128, NP, TS]), Hr=A_("Hr", [128, NP, TS], BF16), Hi=A_("Hi", [128, NP, TS], BF16),
                  ge=[A_(f"ge{i}", [128, 8, TS]) for i in range(3)], sb=A_("sb", [128, 8, TS], BF16), u=A_("u", [128, 8, TS], BF16))
    lanes_b = [mk_lane_b(li) for li in range(int(os.environ.get("KLB", "2")))]

    def final_state(B, tv, dst_re, dst_im):
        Fr_, Fi_ = FT[tv]
        gr = (B.Xr.a[:, :, tv - 1], B.Xr.b); gi = (B.Xi.a[:, :, tv - 1], B.Xi.b)
        cmul(P(hor), P(hoi), P(Fr_), P(Fi_), gr, gi, P(t2), P(t3))
        with nc.allow_non_contiguous_dma("ssm state out"):
            for g2 in range(2):
                sl = slice(64 * g2, 64 * g2 + 64)
                S.dma("sp", dst_re.rearrange("(p g) n -> g n p", g=2)[g2], hor.a[sl, :], reads=[hor.b])
                S.dma("sp", dst_im.rearrange("(p g) n -> g n p", g=2)[g2], hoi.a[sl, :], reads=[hoi.b])

    def ssm_sub(B, ucol, tv, full, fo, pre, post):
        Xr, Xi, tA, tB, tC, tD, Hr, Hi, ge, uT = B.Xr, B.Xi, B.tA, B.tB, B.tC, B.tD, B.Hr, B.Hi, B.ge, B.u
        S.dma("sp", uT.a, Us.a[:, :, ucol:ucol + TS], reads=[Us.b], writes=[uT.b])
        yield
        xb_ = [psA[:, 2 + pp, :].rearrange("p (a h t) -> p a h t", a=8, h=2) for pp in range(4)]
        for tl in range(8):
            for pln in range(2):
                for pp in range(4):
                    mm(xb_[pp][:, tl, pln, :], BT.a[32 * pp:32 * pp + 32, tl, pln, :], uT.a[32 * pp:32 * pp + 32, tl, :],
                       True, True, [BT.b, uT.b], [pb[2 + pp]], inc=(tl == 7 and pln == 1), tile_position=(32 * pp, 0))
        Xr4 = Xr.a.rearrange("p (a b) t -> p a b t", b=4); Xi4 = Xi.a.rearrange("p (a b) t -> p a b t", b=4)
        for pp in range(4):
            act(Xr4[:, :, pp, :], xb_[pp][:, :, 0, :], AF.Copy, [pb[2 + pp]], [Xr.b])
            act(Xi4[:, :, pp, :], xb_[pp][:, :, 1, :], AF.Copy, [pb[2 + pp]], [Xi.b])
        yield
        tt(DVE, tA.a, Er.a, Xr.a, ALU.mult, [Er.b, Xr.b], [tA.b])
        tt(DVE, tB.a, Ei.a, Xi.a, ALU.mult, [Ei.b, Xi.b], [tB.b])
        tt(DVE, tA.a, tA.a, tB.a, ALU.add, [tA.b, tB.b], [tA.b])
        tt(POOL, tC.a, Er.a, Xi.a, ALU.mult, [Er.b, Xi.b], [tC.b])
        tt(POOL, tD.a, Ei.a, Xr.a, ALU.mult, [Ei.b, Xr.b], [tD.b])
        tt(POOL, tC.a, tC.a, tD.a, ALU.subtract, [tC.b, tD.b], [tC.b])
        yield
        if pre is not None:
            pre(B)
        tt(DVE, tA.a[:, :, 0], tA.a[:, :, 0], injr.a, ALU.add, [tA.b, injr.b], [tA.b])
        tt(DVE, tC.a[:, :, 0], tC.a[:, :, 0], inji.a, ALU.add, [tC.b, inji.b], [tC.b])
        S.op(DVE, lambda e: e.tensor_tensor_scan(out=f2(Xr.a), data0=f2(dec0.a), data1=f2(tA.a), initial=0.0,
                                                 op0=ALU.mult, op1=ALU.add), reads=[dec0.b, tA.b], writes=[Xr.b])
        S.op(DVE, lambda e: e.tensor_tensor_scan(out=f2(Xi.a), data0=f2(dec0.a), data1=f2(tC.a), initial=0.0,
                                                 op0=ALU.mult, op1=ALU.add), reads=[dec0.b, tC.b], writes=[Xi.b])
        gr = (Xr.a[:, :, tv - 1], Xr.b); gi = (Xi.a[:, :, tv - 1], Xi.b)
        Lr_, Li_ = LamT[tv]
        cmul(P(injr), P(inji), P(Lr_), P(Li_), gr, gi, P(t2), P(t3))
        if post is not None:
            post(B)
        yield
        if not full:
            return
        tt(DVE, tA.a, Er.a, Xr.a, ALU.mult, [Er.b, Xr.b], [tA.b])
        tt(DVE, tB.a, Ei.a, Xi.a, ALU.mult, [Ei.b, Xi.b], [tB.b])
        tt(DVE, Hr.a, tA.a, tB.a, ALU.subtract, [tA.b, tB.b], [Hr.b])
        tt(POOL, tC.a, Ei.a, Xr.a, ALU.mult, [Ei.b, Xr.b], [tC.b])
        tt(POOL, tD.a, Er.a, Xi.a, ALU.mult, [Er.b, Xi.b], [tD.b])
        tt(POOL, Hi.a, tC.a, tD.a, ALU.add, [tC.b, tD.b], [Hi.b])
        yield
        bank = 1
        yv = psA[:, bank, 0:8 * TS].rearrange("p (a t) -> p a t", a=8)
        for tl in range(8):
            k = 0
            for pp in range(4):
                for pln, Hh in ((0, Hr), (1, Hi)):
                    mm(yv[:, tl, :], CT.a[:, tl, pp, pln, :], Hh.a[:, 4 * tl + pp, :], k == 0, False,
                       [CT.b, Hh.b], [pb[bank]], inc=False)
                    k += 1
            mm(yv[:, tl, :], Dd.a[:, tl, :], uT.a[:, tl, :], False, True, [Dd.b, uT.b], [pb[bank]], inc=(tl == 7))
        act(ge[0].a, yv, AF.Square, [pb[bank]], [ge[0].b])
        ts(DVE, ge[0].a, ge[0].a, 0.044715, 1.0, ALU.mult, ALU.add, [ge[0].b], [ge[0].b])
        tt(DVE, ge[1].a, ge[0].a, yv, ALU.mult, [ge[0].b, pb[bank]], [ge[1].b])
        act(ge[2].a, ge[1].a, AF.Sigmoid, [ge[1].b], [ge[2].b], scale=1.5957691216)
        tt(DVE, B.sb.a, ge[2].a, yv, ALU.mult, [ge[2].b, pb[bank]], [B.sb.b])
        S.dma("sp", Ss.a[:, :, fo:fo + tv], B.sb.a[:, :, :tv], reads=[B.sb.b], writes=[Ss.b])
        yield

    S.op(DVE, lambda e: e.memset(injr.a, 0.0), writes=[injr.b])
    S.op(DVE, lambda e: e.memset(inji.a, 0.0), writes=[inji.b])

    def post_meta(B):
        cp(DVE, injmr.a, injr.a, [injr.b], [injmr.b]); cp(DVE, injmi.a, inji.a, [inji.b], [injmi.b])

    def pre_blend(B):
        for (a_, m_) in ((injr, injmr), (inji, injmi)):
            tt(DVE, a_.a, a_.a, m_.a, ALU.subtract, [a_.b, m_.b], [a_.b])
            tt(DVE, a_.a, a_.a, flg.a, ALU.mult, [a_.b, flg.b], [a_.b])
            tt(DVE, a_.a, a_.a, m_.a, ALU.add, [a_.b, m_.b], [a_.b])

    def pre_smp(j):
        def f(B):
            with nc.allow_non_contiguous_dma("ssm state in"):
                for g2 in range(2):
                    sl = slice(64 * g2, 64 * g2 + 64)
                    S.dma("sp", h0r.a[sl, :], st_re[j].rearrange("(p g) n -> g n p", g=2)[g2], writes=[h0r.b])
                    S.dma("sp", h0i.a[sl, :], st_im[j].rearrange("(p g) n -> g n p", g=2)[g2], writes=[h0i.b])
            cmul(P(injr), P(inji), P(LFr), P(LFi), P(h0r), P(h0i), P(t2), P(t3))
        return f

    fb = [lambda B: ssm_sub(B, 0, 16, True, 2048, None, post_meta)]
    for s_ in range(64):
        fb.append(lambda B, s_=s_: ssm_sub(B, UC_CTX + 32 * s_, 32, False, 0, None, None))
    for s_ in range(64):
        fb.append(lambda B, s_=s_: ssm_sub(B, UC_OWN + 32 * s_, 32, True, 32 * s_, pre_blend if s_ == 0 else None,
                                           (lambda B: final_state(B, 32, so_re.a, so_im.a)) if s_ == 63 else None))
    for j in range(2):
        fb.append(lambda B, j=j: ssm_sub(B, UC_SMP + 32 * j, 32, True, 2064 + 32 * j, pre_smp(j),
                                         lambda B, j=j: final_state(B, 32, ss_re.a[j], ss_im.a[j])))
    run_lanes(fb, lanes_b, int(os.environ.get("KSTAG", "3")))
    S.barrier()
    pbk.close()
    if KSTOP == 1:
        return finish_now()
    p1.close()

    p2 = ExitStack()
    w_kv = alloc(p2, "w_kv", [128, 2, 2048], BF16)
    S.dma("pool", w_kv.a, w_ukv.rearrange("(k p) c -> p k c", p=128), writes=[w_kv.b])
    g_kn = alloc(p2, "g_kn", [128, 128]); S.dma("sp", g_kn.a, bcast_row(k_nope_norm, 128), writes=[g_kn.b])
    cbias = alloc(p2, "cbias", [128, 1]); S.dma("sp", cbias.a, ctx_bias, writes=[cbias.b])
    cmask = alloc(p2, "cmask", [128, 128], BF16); S.dma("pool", cmask.a, chunk_mask, writes=[cmask.b])
    KVs = [alloc(p2, f"KVs{j}", [128, 3, NKS], BF16) for j in range(2)]
    with ExitStack() as ld:
        ctk = [alloc(ld, f"ctk{i}", [128, 384], BF16) for i in range(2)]
        for j in range(2):
            for kb in range(17):
                nk = 128 if kb < 16 else 16
                c_ = ctk[kb % 2]
                if kb < 16:
                    S.dma("pool", c_.a[:, 0:256], cl[j, 128 * kb:128 * kb + 128, :], writes=[c_.b])
                    S.dma("pool", c_.a[:, 256:320], ck[j, 128 * kb:128 * kb + 128, :], writes=[c_.b])
                else:
                    S.dma("pool", c_.a[:16, 0:256], cml[j], writes=[c_.b])
                    S.dma("pool", c_.a[:16, 256:320], cmk[j], writes=[c_.b])
                cp(DVE, c_.a[:nk, 320:384], c_.a[:nk, 256:320], [c_.b], [c_.b])
                pt2 = psT[:, kb % 2, :].rearrange("p (k t) -> p k t", k=8)
                for k in range(3):
                    S.op(PE, lambda e: e.transpose(pt2[:, k, :nk], c_.a[:nk, 128 * k:128 * k + 128], ident.a[:nk, :nk]),
                         reads=[c_.b, ident.b], writes=[ptb[kb % 2]], inc=(k == 2))
                cp(ACT, KVs[j].a[:, :, 128 * kb:128 * kb + nk], pt2[:, 0:3, :nk], [ptb[kb % 2]], [KVs[j].b])
            cp(DVE, KVs[j].a[:, :, 2064:2096], KVso.a[:, :, 32 * j:32 * j + 32], [KVso.b], [KVs[j].b])
    KhT = [alloc(p2, f"KhT{i}", [128, NKP], BF16) for i in range(2)]
    Vh = [alloc(p2, f"Vh{i}", [128, 33, 128], BF16) for i in range(2)]
    KsT = [[alloc(p2, f"KsT{i}{j}", [128, NKS], BF16) for j in range(2)] for i in range(2)]
    Vs = [[alloc(p2, f"Vs{i}{j}", [128, 17, 128], BF16) for j in range(2)] for i in range(2)]
    Qh = [alloc(p2, f"Qh{i}", [128, 2, NF], BF16) for i in range(2)]
    ksq = alloc(p2, "ksq", [128, 4, 128]); kss = alloc(p2, "kss", [128, 4]); ktmp = alloc(p2, "ktmp", [128, 4, 128])
    knb = alloc(p2, "knb", [128, 4, 128], BF16)
    PT = [alloc(p2, f"PT{i}", [128, 512], BF16) for i in range(3)]
    rden = alloc(p2, "rden", [128, 512]); Oh = [alloc(p2, f"Oh{i}", [128, 512], BF16) for i in range(2)]
    pt_i = [0]
    tick = [lambda: None]

    xl = [NS(ksq=alloc(p2, f"xksq{i}", [128, 2, 128]), kss=alloc(p2, f"xkss{i}", [128, 2]), ktmp=alloc(p2, f"xktmp{i}", [128, 2, 128]),
             knb=alloc(p2, f"xknb{i}", [128, 2, 128], BF16), bank=i) for i in range(2)]

    def expand_group(L, KV, kT, Vd, grp, b0, h):
        ng = len(grp); nk = grp[0][1]
        kv = psA[:, L.bank, :].rearrange("p (b c) -> p b c", b=2)
        kb_ = [pb[L.bank]]
        for bi_, (c0, _) in enumerate(grp):
            for k in range(2):
                mm(kv[:nk, bi_, :], KV.a[:, k, c0:c0 + nk], w_kv.a[:, k, 256 * h:256 * h + 256], k == 0, k == 1,
                   [KV.b, w_kv.b], kb_, inc=(k == 1))
        cp(ACT, Vd.a[:nk, b0:b0 + ng, :], kv[:nk, :ng, 128:256], kb_, [Vd.b])
        act(L.ksq.a[:nk, :ng, :], kv[:nk, :ng, 0:128], AF.Square, kb_, [L.ksq.b])
        yield
        S.op(DVE, lambda e: e.tensor_reduce(out=L.kss.a[:nk, :ng], in_=L.ksq.a[:nk, :ng, :], axis=AX.X, op=ALU.add), reads=[L.ksq.b], writes=[L.kss.b])
        r = rstd_of(L.kss.a[:nk, :ng], L.kss.b, ng, 1.0 / 128, nk)
        tt(DVE, L.ktmp.a[:nk, :ng, :], kv[:nk, :ng, 0:128], bc(r.a[:nk, 0:ng], [nk, ng, 128], 2), ALU.mult, kb_ + [r.b], [L.ktmp.b])
        tt(POOL, L.knb.a[:nk, :ng, :], L.ktmp.a[:nk, :ng, :], bc(g_kn.a[:nk, :], [nk, ng, 128], 1), ALU.mult, [L.ktmp.b, g_kn.b], [L.knb.b])
        yield
        ptv = psT[:, 0, 0:512].rearrange("p (k t) -> p k t", k=4)[:, 2 * L.bank:2 * L.bank + 2, :]
        for bi_ in range(ng):
            S.op(PE, lambda e: e.transpose(ptv[:, bi_, :nk], L.knb.a[:nk, bi_, :], ident.a[:nk, :nk]),
                 reads=[L.knb.b, ident.b], writes=[ptb[0]], inc=(bi_ == ng - 1))
        if nk == 128:
            cp(ACT, kT.a[:, grp[0][0]:grp[0][0] + 128 * ng], ptv[:, :ng, :].rearrange("p k t -> p (k t)"), [ptb[0]], [kT.b])
        else:
            cp(ACT, kT.a[:, grp[0][0]:grp[0][0] + nk], ptv[:, 0, :nk], [ptb[0]], [kT.b])
        yield

    def expand_facts(KV, kT, Vd, keyblocks, h):
        out = []; gi = 0
        while gi < len(keyblocks):
            grp = [keyblocks[gi]]
            if keyblocks[gi][1] == 128 and gi + 1 < len(keyblocks) and keyblocks[gi + 1][1] == 128:
                grp.append(keyblocks[gi + 1])
            out.append(lambda L, grp=grp, b0=gi: expand_group(L, KV, kT, Vd, grp, b0, h))
            gi += len(grp)
        return out

    def attend(h, Qt, q0, nq, kT, KV, Vd, blocks, dst_col):
        if os.environ.get("KNOATT"):
            return
        attend_(h, Qt, q0, nq, kT, KV, Vd, blocks, dst_col)

    def attend_(h, Qt, q0, nq, kT, KV, Vd, blocks, dst_col):
        hr = 64 * (h % 2)
        ob, db = 4, 5
        O = psA[:, ob, :]; Dn = psA[:, db, :]
        nb = len(blocks)
        sbanks = [(psA[:, 2, :], pb[2]), (psA[:, 3, :], pb[3]), (psX, pxb)]

        def emit_s(bi_):
            c0, nk, vb, qlo, bias, msk = blocks[bi_]
            Sps, sb_ = sbanks[bi_ % 3]
            mm(Sps[:nk, qlo:nq], kT.a[:, c0:c0 + nk], Qt.a[:, 0, q0 + qlo:q0 + nq], True, False, [kT.b, Qt.b], [sb_], inc=False)
            mm(Sps[:nk, qlo:nq], KV.a[hr:hr + 64, 2, c0:c0 + nk], Qt.a[hr:hr + 64, 1, q0 + qlo:q0 + nq], False, True,
               [KV.b, Qt.b], [sb_], inc=True)

        emit_s(0)
        if nb > 1:
            emit_s(1)
        for bi_, (c0, nk, vb, qlo, bias, msk) in enumerate(blocks):
            if bi_ + 2 < nb:
                emit_s(bi_ + 2)
            tick[0]()
            Sps, sb_ = sbanks[bi_ % 3]
            p_ = PT[pt_i[0] % 3]; pt_i[0] += 1
            if bias is not None:
                act(p_.a[:nk, qlo:nq], Sps[:nk, qlo:nq], AF.Exp, [sb_, bias[1]], [p_.b], scale=SCALE, bias=bias[0][:nk, :])
            else:
                act(p_.a[:nk, qlo:nq], Sps[:nk, qlo:nq], AF.Exp, [sb_], [p_.b], scale=SCALE)
            if msk:
                tt(POOL, p_.a[:nk, qlo:qlo + 128], p_.a[:nk, qlo:qlo + 128], cmask.a[:nk, :], ALU.mult, [p_.b, cmask.b], [p_.b])
            mm(O[:, qlo:nq], Vd.a[:nk, vb, :], p_.a[:nk, qlo:nq], bi_ == 0, bi_ == nb - 1, [Vd.b, p_.b], [pb[ob]], inc=False)
            mm(Dn[:, qlo:nq], ones.a[:nk, :], p_.a[:nk, qlo:nq], bi_ == 0, bi_ == nb - 1, [ones.b, p_.b], [pb[db]], inc=True)
        S.op(DVE, lambda e: e.reciprocal(out=rden.a[:, :nq], in_=Dn[:, :nq]), reads=[pb[db]], writes=[rden.b])
        o_ = Oh[h % 2]
        tt(DVE, o_.a[:, :nq], O[:, :nq], rden.a[:, :nq], ALU.mult, [pb[ob], rden.b], [o_.b])
        S.dma("sp", Os.a[:, h, dst_col:dst_col + nq], o_.a[:, :nq], reads=[o_.b], writes=[Os.b])

    pkeys = [(128 * i, 128) for i in range(32)] + [(4096, 16)]
    skeys = [(128 * i, 128) for i in range(16)] + [(2048, 48)]

    def prep_head(h):
        hb = h % 2
        S.dma("sp", Qh[hb].a[:, 0, :], Qs.a[:, h, :], reads=[Qs.b], writes=[Qh[hb].b])
        S.dma("sp", Qh[hb].a[:, 1, :], Qs.a[:, 8 + h // 2, :], reads=[Qs.b], writes=[Qh[hb].b])
        fx = expand_facts(KVp, KhT[hb], Vh[hb], pkeys, h)
        for j in range(2):
            fx += expand_facts(KVs[j], KsT[hb][j], Vs[hb][j], skeys, h)
        run_lanes(fx, xl, 1)
        yield

    def drain(g):
        for _ in g:
            pass
    nxt = [None]
    cnt = [0]

    def do_tick():
        cnt[0] += 1
        if nxt[0] is not None and cnt[0] % 2 == 0:
            try:
                next(nxt[0])
            except StopIteration:
                nxt[0] = None
    tick[0] = do_tick
    drain(prep_head(0))
    for h in range(8):
        hb = h % 2
        if os.environ.get("KINT"):
            nxt[0] = prep_head(h + 1) if h < 7 else None
        elif h > 0:
            drain(prep_head(h))
        for sbi in range(4):
            blocks = [(128 * i, 128, i, 0, (cbias.a, cbias.b), False) for i in range(16)]
            blocks += [(4096, 16, 32, 0, None, False)]
            for ob_ in range(4 * sbi + 4):
                j_ = ob_ - 4 * sbi
                qlo = 128 * j_ if j_ > 0 else 0
                blocks.append((2048 + 128 * ob_, 128, 16 + ob_, qlo, None, j_ >= 0))
            attend(h, Qh[hb], 512 * sbi, 512, KhT[hb], KVp, Vh[hb], blocks, 512 * sbi)
        attend(h, Qh[hb], 2048, 16, KhT[hb], KVp, Vh[hb], [(4096, 16, 32, 0, None, False)], 2048)
        for j in range(2):
            blocks = [(c0, nk, i, 0, None, False) for i, (c0, nk) in enumerate(skeys)]
            attend(h, Qh[hb], 2064 + 32 * j, 32, KsT[hb][j], KVs[j], Vs[hb][j], blocks, 2064 + 32 * j)
        if nxt[0] is not None:
            drain(nxt[0])
            nxt[0] = None
    S.barrier()
    if KSTOP == 2:
        return finish_now()
    p2.close()
    pkv.close()

    p3 = ExitStack()
    g_mix3 = alloc(p3, "g_mix3", [128, D]); S.dma("sp", g_mix3.a, bcast_row(norm_mix, D), writes=[g_mix3.b])
    w_g = alloc(p3, "w_g", [128, 8, 2048], BF16); S.dma("pool", w_g.a, w_in_v[:, :, 1728:3776], writes=[w_g.b])
    w_oa = alloc(p3, "w_oa", [128, 8, D], BF16); S.dma("pool", w_oa.a, w_o.rearrange("(k p) c -> p k c", p=128), writes=[w_oa.b])
    w_gv = alloc(p3, "w_gv", [128, 8, D], BF16); S.dma("pool", w_gv.a, w_glu_v.rearrange("(k p) c -> p k c", p=128), writes=[w_gv.b])
    w_gg = alloc(p3, "w_gg", [128, 8, D], BF16); S.dma("pool", w_gg.a, w_glu_g.rearrange("(k p) c -> p k c", p=128), writes=[w_gg.b])
    w_ot = alloc(p3, "w_ot", [128, 8, D], BF16); S.dma("pool", w_ot.a, w_out.rearrange("(k p) c -> p k c", p=128), writes=[w_ot.b])
    xb3 = [alloc(p3, f"xb3{i}", [128, D]) for i in range(4)]
    junk3 = alloc(p3, "junk3", [128, D], BF16); ss3 = alloc(p3, "ss3", [128, 4]); xs3 = alloc(p3, "xs3", [128, D], BF16)
    xnT3 = alloc(p3, "xnT3", [128, 8, 512], BF16)
    OT3 = alloc(p3, "OT3", [128, 8, 512], BF16); ST3 = alloc(p3, "ST3", [128, 8, 512], BF16)
    mT = alloc(p3, "mT", [128, 8, 512], BF16)
    sga = alloc(p3, "sga", [128, 512]); sgg = alloc(p3, "sgg", [128, 512]); sgb = alloc(p3, "sgb", [128, 512])
    e1 = alloc(p3, "e1", [128, 512]); e2 = alloc(p3, "e2", [128, 512])
    h1t = [alloc(p3, f"h1t{i}", [128, D]) for i in range(2)]
    sbs = [([(x_own[512 * s_ + 128 * b_:512 * s_ + 128 * b_ + 128, :], 128) for b_ in range(4)], 512 * s_) for s_ in range(4)]
    sbs.append(([(x_meta, 16), (x_smp, 64)], 2048))
    for (blks, fo) in sbs:
        nsb = sum(n for _, n in blks)
        off = 0
        for bi_, (src, nt) in enumerate(blks):
            x = xb3[bi_]
            S.dma("sp", x.a[:nt, :], src, writes=[x.b])
            act(junk3.a[:nt, :], x.a[:nt, :], AF.Square, [x.b], [junk3.b, ss3.b], accum_out=ss3.a[:nt, bi_:bi_ + 1])
            r = rstd_of(ss3.a[:nt, bi_:bi_ + 1], ss3.b, 1, 1.0 / D, nt)
            S.op(DVE, lambda e: e.scalar_tensor_tensor(out=xs3.a[:nt, :], in0=x.a[:nt, :], scalar=r.a[:nt, 0:1], in1=g_mix3.a[:nt, :],
                                                       op0=ALU.mult, op1=ALU.mult), reads=[x.b, r.b, g_mix3.b], writes=[xs3.b])
            pt = psT[:, bi_ % 2, :].rearrange("p (k t) -> p k t", k=8)
            for k in range(8):
                S.op(PE, lambda e: e.transpose(pt[:, k, :nt], xs3.a[:nt, 128 * k:128 * k + 128], ident.a[:nt, :nt]),
                     reads=[xs3.b, ident.b], writes=[ptb[bi_ % 2]], inc=(k == 7))
            cp(ACT, xnT3.a[:, :, off:off + nt], pt[:, :, :nt], [ptb[bi_ % 2]], [xnT3.b])
            off += nt
        S.dma("sp", OT3.a[:, :, :nsb], Os.a[:, :, fo:fo + nsb], reads=[Os.b], writes=[OT3.b])
        S.dma("sp", ST3.a[:, :, :nsb], Ss.a[:, :, fo:fo + nsb], reads=[Ss.b], writes=[ST3.b])
        for m in range(8):
            cs = slice(128 * m, 128 * m + 128)
            specs = [(0, w_g, 0, xnT3), (1, w_g, 1024, xnT3), (2, w_oa, 0, OT3), (3, w_gv, 0, ST3), (4, w_gg, 0, ST3)]
            for (bk, wt, wo, at) in specs:
                for k in range(8):
                    mm(psA[:, bk, :nsb], wt.a[:, k, wo + 128 * m:wo + 128 * m + 128], at.a[:, k, :nsb], k == 0, k == 7,
                       [wt.b, at.b], [pb[bk]], inc=(k == 7))
            act(sga.a[:, :nsb], psA[:, 0, :nsb], AF.Sigmoid, [pb[0]], [sga.b])
            act(sgb.a[:, :nsb], psA[:, 1, :nsb], AF.Sigmoid, [pb[1]], [sgb.b])
            act(sgg.a[:, :nsb], psA[:, 4, :nsb], AF.Sigmoid, [pb[4]], [sgg.b])
            tt(DVE, e1.a[:, :nsb], sga.a[:, :nsb], psA[:, 2, :nsb], ALU.mult, [sga.b, pb[2]], [e1.b])
            tt(DVE, e2.a[:, :nsb], sgg.a[:, :nsb], psA[:, 3, :nsb], ALU.mult, [sgg.b, pb[3]], [e2.b])
            tt(POOL, e2.a[:, :nsb], e2.a[:, :nsb], sgb.a[:, :nsb], ALU.mult, [e2.b, sgb.b], [e2.b])
            tt(POOL, mT.a[:, m, :nsb], e1.a[:, :nsb], e2.a[:, :nsb], ALU.add, [e1.b, e2.b], [mT.b])
        off = 0
        for bi_, (src, nt) in enumerate(blks):
            hp = psA[:, 0:2, :].rearrange("p a b -> p (a b)") if bi_ % 2 == 0 else psA[:, 2:4, :].rearrange("p a b -> p (a b)")
            hb_ = [pb[0], pb[1]] if bi_ % 2 == 0 else [pb[2], pb[3]]
            for n in range(2):
                for k in range(8):
                    mm(hp[:nt, 512 * n:512 * n + 512], mT.a[:, k, off:off + nt], w_ot.a[:, k, 512 * n:512 * n + 512], k == 0, k == 7,
                       [mT.b, w_ot.b], [hb_[n]], inc=(k == 7))
            ht = h1t[bi_ % 2]
            tt(DVE, ht.a[:nt, :], hp[:nt, :], xb3[bi_].a[:nt, :], ALU.add, hb_ + [xb3[bi_].b], [ht.b])
            S.dma("sp", H1.a[fo + off:fo + off + nt, :], ht.a[:nt, :], reads=[ht.b], writes=[H1.b])
            off += nt
    S.barrier()
    if KSTOP == 3:
        return finish_now()
    p3.close()

    p4 = ExitStack()
    g_mlp = alloc(p4, "g_mlp", [128, D]); S.dma("sp", g_mlp.a, bcast_row(norm_mlp, D), writes=[g_mlp.b])
    w_upb = alloc(p4, "w_upb", [128, 8, 4096], BF16)
    w_dnb = alloc(p4, "w_dnb", [128, 32, D], BF16)
    w_up_v = w_up.rearrange("(k p) c -> p k c", p=128); w_dn_v = w_down.rearrange("(k p) c -> p k c", p=128)
    for q_ in range(4):
        S.dma("pool", w_upb.a[:, :, 1024 * q_:1024 * q_ + 1024], w_up_v[:, :, 1024 * q_:1024 * q_ + 1024], writes=[w_upb.b])
    for q_ in range(4):
        S.dma("pool", w_dnb.a[:, 8 * q_:8 * q_ + 8, :], w_dn_v[:, 8 * q_:8 * q_ + 8, :], writes=[w_dnb.b])
    hb4 = [alloc(p4, f"hb4{i}", [128, D]) for i in range(4)]
    junk4 = alloc(p4, "junk4", [128, D], BF16); ss4 = alloc(p4, "ss4", [128, 4]); xs4 = alloc(p4, "xs4", [128, D], BF16)
    xnT4 = alloc(p4, "xnT4", [128, 8, 256], BF16); a2T = alloc(p4, "a2T", [128, 32, 256], BF16)
    rl = [alloc(p4, f"rl{i}", [128, 256]) for i in range(2)]
    yt = [alloc(p4, f"yt{i}", [128, D]) for i in range(2)]
    sb4 = [[(256 * s_ + 128 * b_, 128, y_own, 256 * s_ + 128 * b_) for b_ in range(2)] for s_ in range(8)]
    sb4.append([(2064, 64, y_smp, 0)])
    it4 = [0]
    for blks in sb4:
        nsb = sum(b_[1] for b_ in blks)
        off = 0
        for bi_, (f0, nt, _, _) in enumerate(blks):
            hh = hb4[(it4[0] * 2 + bi_) % 4]
            S.dma("sp", hh.a[:nt, :], H1.a[f0:f0 + nt, :], reads=[H1.b], writes=[hh.b])
            act(junk4.a[:nt, :], hh.a[:nt, :], AF.Square, [hh.b], [junk4.b, ss4.b], accum_out=ss4.a[:nt, bi_:bi_ + 1])
            r = rstd_of(ss4.a[:nt, bi_:bi_ + 1], ss4.b, 1, 1.0 / D, nt)
            S.op(DVE, lambda e: e.scalar_tensor_tensor(out=xs4.a[:nt, :], in0=hh.a[:nt, :], scalar=r.a[:nt, 0:1], in1=g_mlp.a[:nt, :],
                                                       op0=ALU.mult, op1=ALU.mult), reads=[hh.b, r.b, g_mlp.b], writes=[xs4.b])
            pt = psT[:, bi_ % 2, :].rearrange("p (k t) -> p k t", k=8)
            for k in range(8):
                S.op(PE, lambda e: e.transpose(pt[:, k, :nt], xs4.a[:nt, 128 * k:128 * k + 128], ident.a[:nt, :nt]),
                     reads=[xs4.b, ident.b], writes=[ptb[bi_ % 2]], inc=(k == 7))
            cp(ACT, xnT4.a[:, :, off:off + nt], pt[:, :, :nt], [ptb[bi_ % 2]], [xnT4.b])
            off += nt
        for f in range(32):
            bk = 4 + f % 2
            for k in range(8):
                mm(psA[:, bk, :nsb], w_upb.a[:, k, 128 * f:128 * f + 128], xnT4.a[:, k, :nsb], k == 0, k == 7,
                   [w_upb.b, xnT4.b], [pb[bk]], inc=(k == 7))
            r_ = rl[f % 2]
            act(r_.a[:, :nsb], psA[:, bk, :nsb], AF.Relu, [pb[bk]], [r_.b])
            tt(DVE if f % 2 == 0 else POOL, a2T.a[:, f, :nsb], r_.a[:, :nsb], r_.a[:, :nsb], ALU.mult, [r_.b], [a2T.b])
        off = 0
        for bi_, (f0, nt, dst, drow) in enumerate(blks):
            hh = hb4[(it4[0] * 2 + bi_) % 4]
            yp = psA[:, 0:2, :].rearrange("p a b -> p (a b)") if bi_ % 2 == 0 else psA[:, 2:4, :].rearrange("p a b -> p (a b)")
            yb_ = [pb[0], pb[1]] if bi_ % 2 == 0 else [pb[2], pb[3]]
            for n in range(2):
                for f in range(32):
                    mm(yp[:nt, 512 * n:512 * n + 512], a2T.a[:, f, off:off + nt], w_dnb.a[:, f, 512 * n:512 * n + 512], f == 0, f == 31,
                       [a2T.b, w_dnb.b], [yb_[n]], inc=(f == 31))
            y_ = yt[bi_ % 2]
            tt(DVE, y_.a[:nt, :], yp[:nt, :], hh.a[:nt, :], ALU.add, yb_ + [hh.b], [y_.b])
            S.dma("sp", dst.a[drow:drow + nt, :], y_.a[:nt, :], reads=[y_.b], writes=[dst.b])
            off += nt
        it4[0] += 1
    S._wait(S.sp, [o.b for o in outs], [o.b for o in outs])
    S.barrier()
    p4.close()
    top.close()
    return nc


_NC = None


def _rope_table(pos):
    half = 32
    inv = (10000.0 ** (-np.arange(half, dtype=np.float32) / half)).astype(np.float32)
    ang = pos.astype(np.float32)[:, None] * inv[None, :]
    return np.concatenate([np.cos(ang), np.sin(ang)], axis=1).astype(np.float32)


def kernel(**inp):
    global _NC
    f = lambda a: np.ascontiguousarray(np.asarray(a, dtype=np.float32))
    if _NC is None:
        _NC = build_program()
    nc = _NC
    xp = f(inp["x_prompt"]); xsm = f(inp["x_sample"])
    shared = {
        "x_meta": f(inp["meta_tokens"]),
        "rope_meta": _rope_table(np.arange(16) - 16), "rope_ctx": _rope_table(np.arange(2048)),
        "rope_smp": np.tile(_rope_table(2048 + np.arange(32)), (2, 1)),
        "norm_mix": f(inp["norm_mix"]), "w_in": f(inp["w_in"][0]), "q_lora_norm": f(inp["q_lora_norm"]),
        "w_uq": f(inp["w_uq"][0]).reshape(384, 1536), "q_nope_norm": f(inp["q_nope_norm"]), "q_rope_norm": f(inp["q_rope_norm"]),
        "kv_lora_norm": f(inp["kv_lora_norm"]), "k_rope_norm": f(inp["k_rope_norm"]),
        "w_ukv": f(inp["w_ukv"][0]).reshape(256, 2048), "k_nope_norm": f(inp["k_nope_norm"]),
        "w_o_attn": f(inp["w_o_attn"][0]).reshape(1024, 1024),
        "ssm_a_re": f(inp["ssm_a_re"][0]), "ssm_a_im": f(inp["ssm_a_im"][0]), "ssm_log_dt": f(inp["ssm_log_dt"][0]),
        "ssm_b_re": f(inp["ssm_b_re"][0]), "ssm_b_im": f(inp["ssm_b_im"][0]),
        "ssm_c_re": f(inp["ssm_c_re"][0]), "ssm_c_im": f(inp["ssm_c_im"][0]), "ssm_d": f(inp["ssm_d"][0]),
        "w_glu_v": f(inp["w_glu_v"][0]), "w_glu_g": f(inp["w_glu_g"][0]), "w_out": f(inp["w_out"][0]),
        "norm_mlp": f(inp["norm_mlp"]), "w_mlp_up": f(inp["w_mlp_up"][0]), "w_mlp_down": f(inp["w_mlp_down"][0]),
    }
    pidx = np.arange(128)
    mask_b = np.zeros((128, 2), np.float32); mask_b[pidx, (pidx // 16) % 2] = 1.0
    mask_c = np.zeros((128, 4, 8), np.float32)
    for pp in range(4):
        mask_c[pidx, pp, 2 * pp + pidx // 64] = 1.0
    kk = np.arange(128)[:, None]; qq = np.arange(128)[None, :]
    chunk_mask = ((kk // 64) <= (qq // 64)).astype(np.float32)
    shared.update(mask_b=mask_b, mask_c=mask_c.reshape(128, 32), chunk_mask=chunk_mask)
    zeros_x = np.zeros((2048, 1024), np.float32)
    in_maps = []
    for c in range(8):
        b, half = c // 2, c % 2
        m = dict(shared)
        m["x_own"] = np.ascontiguousarray(xp[b, 2048 * half:2048 * half + 2048])
        m["x_ctx"] = np.ascontiguousarray(xp[b, 0:2048]) if half else zeros_x
        m["x_smp"] = np.ascontiguousarray(xsm[2 * c:2 * c + 2].reshape(64, 1024))
        m["cl"] = f(inp["cache_latent"][0, 2 * c:2 * c + 2]); m["cml"] = f(inp["cache_meta_latent"][0, 2 * c:2 * c + 2])
        m["ck"] = f(inp["cache_krope"][0, 2 * c:2 * c + 2]); m["cmk"] = f(inp["cache_meta_krope"][0, 2 * c:2 * c + 2])
        m["st_re"] = f(inp["state_ssm_re"][0, 2 * c:2 * c + 2]); m["st_im"] = f(inp["state_ssm_im"][0, 2 * c:2 * c + 2])
        m["rope_own"] = _rope_table(2048 * half + np.arange(2048))
        m["ctx_bias"] = np.full((128, 1), 0.0 if half else -30000.0, np.float32)
        m["flagb"] = np.full((128, 32), float(half), np.float32)
        in_maps.append(m)
    res = run_bass_kernel_spmd(nc, in_maps, core_ids=list(range(8)))
    R = res.results
    cat = lambda key: np.stack([np.concatenate([R[2 * b][key], R[2 * b + 1][key]], axis=0) for b in range(4)])
    y_prompt = cat("y_own")
    y_sample = np.concatenate([R[c]["y_smp"].reshape(2, 32, 1024) for c in range(8)], axis=0)
    lat_p = cat("lat_own")[None]; kr_p = cat("kr_own")[None]
    mlat = np.stack([R[2 * b]["lat_meta"] for b in range(4)])[None]
    mkr = np.stack([R[2 * b]["kr_meta"] for b in range(4)])[None]
    sre = np.stack([R[2 * b + 1]["so_re"] for b in range(4)])[None]
    sim = np.stack([R[2 * b + 1]["so_im"] for b in range(4)])[None]
    lat_s = np.concatenate([R[c]["lat_smp"].reshape(2, 32, 256) for c in range(8)], axis=0)[None]
    kr_s = np.concatenate([R[c]["kr_smp"].reshape(2, 32, 64) for c in range(8)], axis=0)[None]
    sre_s = np.concatenate([R[c]["ss_re"] for c in range(8)], axis=0)[None]
    sim_s = np.concatenate([R[c]["ss_im"] for c in range(8)], axis=0)[None]
    outs = (y_prompt, y_sample, lat_p, kr_p, mlat, mkr, sre, sim, lat_s, kr_s, sre_s, sim_s)
    return tuple(np.ascontiguousarray(o, dtype=np.float32) for o in outs)
```

```python
import math
from contextlib import ExitStack
import numpy as np
import concourse.bass as bass
import concourse.mybir as mybir
from concourse.bass_utils import run_bass_kernel_spmd

F32 = mybir.dt.float32
BF16 = mybir.dt.bfloat16
AF = mybir.ActivationFunctionType
ALU = mybir.AluOpType
AX = mybir.AxisListType

D = 1024
NOWN = 2048
NF = 2128
NKP = 4112
NKS = 2096
EPS = 1e-6
SCALE = 192 ** -0.5
TS = 32


class Buf:
    __slots__ = ("w", "r", "excl")

    def __init__(self, excl=False):
        self.w = None
        self.r = []
        self.excl = excl


class Eng:
    def __init__(self, nc, name, e, step=1, pe=False):
        self.name = name
        self.e = e
        self.sem = nc.alloc_semaphore("s_" + name)
        self.n = 0
        self.step = step
        self.pe = pe
        self.seen = {}


class Sched:
    def __init__(self, nc):
        self.nc = nc
        self.pe = Eng(nc, "pe", nc.tensor, pe=True)
        self.act = Eng(nc, "act", nc.scalar)
        self.dve = Eng(nc, "dve", nc.vector)
        self.pool = Eng(nc, "pool", nc.gpsimd)
        self.sp = Eng(nc, "sp", nc.sync)
        self.d_sp = [Eng(nc, f"dsp{i}", None, step=16) for i in range(24)]
        self.d_pool = [Eng(nc, f"dpl{i}", None, step=16) for i in range(8)]
        self.qi = {"sp": 0, "pool": 0}
        self.real = [self.pe, self.act, self.dve, self.pool, self.sp]
        self.all = self.real + self.d_sp + self.d_pool

    def _wait(self, eng, reads, writes):
        best = {}
        for b in reads:
            if b.w is not None:
                e2, seq = b.w
                if best.get(e2, 0) < seq:
                    best[e2] = seq
            if b.excl:
                for (e2, seq) in b.r:
                    if e2 is not eng and best.get(e2, 0) < seq:
                        best[e2] = seq
        for b in writes:
            if b.w is not None:
                e2, seq = b.w
                if best.get(e2, 0) < seq:
                    best[e2] = seq
            for (e2, seq) in b.r:
                if best.get(e2, 0) < seq:
                    best[e2] = seq
        for e2, seq in best.items():
            if e2 is eng and eng.pe:
                continue
            if eng.seen.get(e2, 0) >= seq:
                continue
            eng.e.wait_ge(e2.sem, seq * e2.step)
            eng.seen[e2] = seq

    def _mark(self, tag, reads, writes):
        for b in reads:
            if len(b.r) > 6:
                m = {}
                for (e2, s) in b.r:
                    if m.get(e2, 0) < s:
                        m[e2] = s
                b.r = list(m.items())
            b.r.append(tag)
        for b in writes:
            b.w = tag
            b.r = []

    dead = False
    opc = 0
    limit = 10 ** 9

    trace = []

    def op(self, eng, fn, reads=(), writes=(), inc=True):
        Sched.opc += 1
        if Sched.trace is not None:
            import sys as _s
            f_ = _s._getframe(1); ln = []
            while f_ is not None and len(ln) < 4:
                ln.append(f_.f_lineno); f_ = f_.f_back
            Sched.trace.append((Sched.opc, eng.name, ln))
        if Sched.opc > Sched.limit:
            Sched.dead = True
        if Sched.dead:
            if eng.pe and not inc:
                return None
            if eng.pe:
                return None
            return None
        self._wait(eng, reads, writes)
        inst = fn(eng.e)
        if inc:
            inst.then_inc(eng.sem, 1)
            eng.n += 1
            tag = (eng, eng.n)
        else:
            tag = (eng, eng.n + 1)
        self._mark(tag, reads, writes)
        return inst

    def dma(self, q, out, in_, reads=(), writes=()):
        Sched.opc += 1
        if Sched.opc > Sched.limit:
            Sched.dead = True
        if Sched.dead:
            return None
        eng, ring = (self.sp, self.d_sp) if q == "sp" else (self.pool, self.d_pool)
        d = ring[self.qi[q] % len(ring)]
        self.qi[q] += 1
        self._wait(eng, reads, writes)
        if d.n > 0 and eng.seen.get(d, 0) < d.n:
            eng.e.wait_ge(d.sem, d.n * 16)
            eng.seen[d] = d.n
        inst = eng.e.dma_start(out=out, in_=in_)
        inst.then_inc(d.sem, 16)
        d.n += 1
        self._mark((d, d.n), reads, writes)
        return inst

    def barrier(self):
        for e in self.real:
            for o in self.all:
                if o is e or o.n == 0:
                    continue
                if e.seen.get(o, 0) >= o.n:
                    continue
                e.e.wait_ge(o.sem, o.n * o.step)
                e.seen[o] = o.n


class T:
    def __init__(self, ap):
        self.a = ap
        self.b = Buf()


def build_program():
    import os
    nc = bass.Bass("TRN2", target_bir_lowering=False)
    S = Sched(nc)
    PE, ACT, DVE, POOL = S.pe, S.act, S.dve, S.pool

    def din(name, shape):
        return nc.dram_tensor(name, list(shape), F32, kind="ExternalInput").ap()

    def dout(name, shape):
        return T(nc.dram_tensor(name, list(shape), F32, kind="ExternalOutput").ap())

    x_own = din("x_own", (NOWN, D)); x_ctx = din("x_ctx", (NOWN, D))
    x_meta = din("x_meta", (16, D)); x_smp = din("x_smp", (64, D))
    cl = din("cl", (2, 2048, 256)); cml = din("cml", (2, 16, 256))
    ck = din("ck", (2, 2048, 64)); cmk = din("cmk", (2, 16, 64))
    st_re = din("st_re", (2, 64, 64)); st_im = din("st_im", (2, 64, 64))
    rope_own = din("rope_own", (NOWN, 64)); rope_ctx = din("rope_ctx", (NOWN, 64))
    rope_meta = din("rope_meta", (16, 64)); rope_smp = din("rope_smp", (64, 64))
    ctx_bias = din("ctx_bias", (128, 1)); flagb = din("flagb", (128, 32))
    mask_b = din("mask_b", (128, 2)); mask_c = din("mask_c", (128, 32)); chunk_mask = din("chunk_mask", (128, 128))
    norm_mix = din("norm_mix", (1, D)); w_in = din("w_in", (D, 3776))
    q_lora_norm = din("q_lora_norm", (1, 384)); w_uq = din("w_uq", (384, 1536))
    q_nope_norm = din("q_nope_norm", (1, 128)); q_rope_norm = din("q_rope_norm", (1, 64))
    kv_lora_norm = din("kv_lora_norm", (1, 256)); k_rope_norm = din("k_rope_norm", (1, 64))
    w_ukv = din("w_ukv", (256, 2048)); k_nope_norm = din("k_nope_norm", (1, 128))
    w_o = din("w_o_attn", (1024, D))
    a_re = din("ssm_a_re", (64, 64)); a_im = din("ssm_a_im", (64, 64)); log_dt = din("ssm_log_dt", (64,))
    b_re = din("ssm_b_re", (64, 64, 16)); b_im = din("ssm_b_im", (64, 64, 16))
    c_re = din("ssm_c_re", (64, 16, 64)); c_im = din("ssm_c_im", (64, 16, 64))
    ssm_d = din("ssm_d", (D,))
    w_glu_v = din("w_glu_v", (D, D)); w_glu_g = din("w_glu_g", (D, D)); w_out = din("w_out", (D, D))
    norm_mlp = din("norm_mlp", (1, D)); w_up = din("w_mlp_up", (D, 4096)); w_down = din("w_mlp_down", (4096, D))

    y_own = dout("y_own", (NOWN, D)); y_smp = dout("y_smp", (64, D))
    lat_own = dout("lat_own", (NOWN, 256)); kr_own = dout("kr_own", (NOWN, 64))
    lat_meta = dout("lat_meta", (16, 256)); kr_meta = dout("kr_meta", (16, 64))
    so_re = dout("so_re", (64, 64)); so_im = dout("so_im", (64, 64))
    lat_smp = dout("lat_smp", (64, 256)); kr_smp = dout("kr_smp", (64, 64))
    ss_re = dout("ss_re", (2, 64, 64)); ss_im = dout("ss_im", (2, 64, 64))
    outs = [y_own, y_smp, lat_own, kr_own, lat_meta, kr_meta, so_re, so_im, lat_smp, kr_smp, ss_re, ss_im]

    SK = os.environ.get("KSCR", "ExternalOutput")
    Qs = T(nc.dram_tensor("Qs", [128, 12, NF], BF16, kind=SK).ap())
    Ss = T(nc.dram_tensor("Ss", [128, 8, NF], BF16, kind=SK).ap())
    Os = T(nc.dram_tensor("Os", [128, 8, NF], BF16, kind=SK).ap())
    H1 = T(nc.dram_tensor("H1", [NF, D], F32, kind=SK).ap())
    Us = T(nc.dram_tensor("Us", [128, 8, 32 + 4096 + 64], BF16, kind=SK).ap())

    psA = nc.alloc_psum_tensor("psA", [128, 6, 512], F32).ap()
    psT = nc.alloc_psum_tensor("psT", [128, 2, 1024], BF16).ap()
    psX = psT[:, 1, :].bitcast(F32)
    EXC = os.environ.get("KEXCL", "0") == "1"
    pb = [Buf(excl=EXC) for _ in range(6)]
    ptb = [Buf(excl=EXC) for _ in range(2)]
    pxb = ptb[1]

    def bcast_row(src, n):
        return bass.AP(src.tensor, 0, [[0, 128], [1, n]])

    def mm(out, lhsT, rhs, start, stop, R, W, inc, **kw):
        S.op(PE, lambda e: e.matmul(out, lhsT, rhs, start=start, stop=stop, **kw), reads=R, writes=W, inc=inc)

    def tt(eng, out, a, b, op, R, W):
        S.op(eng, lambda e: e.tensor_tensor(out=out, in0=a, in1=b, op=op), reads=R, writes=W)

    def ts(eng, out, a, s1, s2, op0, op1, R, W):
        S.op(eng, lambda e: e.tensor_scalar(out=out, in0=a, scalar1=s1, scalar2=s2, op0=op0, op1=op1), reads=R, writes=W)

    def ts1(eng, out, a, s1, op, R, W):
        S.op(eng, lambda e: e.tensor_single_scalar(out=out, in_=a, scalar=s1, op=op), reads=R, writes=W)

    def act(out, in_, func, R, W, **kw):
        if "accum_out" in kw:
            ao = kw["accum_out"]
            S.op(DVE, lambda e: e.memset(ao, 0.0), writes=[W[-1]])
        S.op(ACT, lambda e: e.activation(out=out, in_=in_, func=func, **kw), reads=R, writes=W)

    def cp(eng, out, in_, R, W):
        if eng is ACT:
            S.op(eng, lambda e: e.activation(out=out, in_=in_, func=AF.Copy), reads=R, writes=W)
        else:
            S.op(eng, lambda e: e.tensor_copy(out=out, in_=in_), reads=R, writes=W)

    def bc(ap, shape, axis):
        return ap.unsqueeze(axis).broadcast_to(list(shape))

    top = ExitStack()
    KSTOP = int(os.environ.get("KSTOP", "9"))
    Sched.limit = int(os.environ.get("KLIMIT", str(10 ** 9)))
    Sched.opc = 0
    Sched.dead = False

    def finish_now():
        Sched.dead = False
        Sched.limit = 10 ** 9
        for _ in range(int(os.environ.get("KPAD", "0"))):
            if os.environ.get("KPADE", "dve") == "dve":
                S.op(DVE, lambda e: e.memset(halfpi.a, 1.5), writes=[halfpi.b])
            else:
                S.op(ACT, lambda e: e.activation(out=halfpi.a, in_=halfpi.a, func=AF.Copy), reads=[halfpi.b], writes=[halfpi.b])
        S._wait(S.sp, [o.b for o in outs], [o.b for o in outs])
        S.barrier()
        return nc

    def alloc(stack, name, shape, dt=F32):
        return T(stack.enter_context(nc.sbuf_tensor(name, list(shape), dt)).ap())

    ident = alloc(top, "ident", [128, 128], BF16)
    identf = alloc(top, "identf", [128, 128], F32)
    ones = alloc(top, "ones", [128, 128], BF16)
    halfpi = alloc(top, "halfpi", [128, 1])
    for t_ in (ident, identf):
        S.op(POOL, lambda e: e.memset(t_.a, 1.0), writes=[t_.b])
        S.op(POOL, lambda e: e.affine_select(out=t_.a, in_=t_.a, pattern=[[-1, 128]], compare_op=ALU.is_equal,
                                             fill=0.0, base=0, channel_multiplier=1), reads=[t_.b], writes=[t_.b])
    S.op(POOL, lambda e: e.memset(ones.a, 1.0), writes=[ones.b])
    S.op(POOL, lambda e: e.memset(halfpi.a, math.pi / 2), writes=[halfpi.b])
    rs_ring = [alloc(top, f"rs{i}", [128, 16]) for i in range(4)]
    rs_i = [0]

    def rstd_of(ss_ap, ssb, n, inv_d, nt):
        r = rs_ring[rs_i[0] % 4]; rs_i[0] += 1
        ts(DVE, r.a[:nt, 0:n], ss_ap, inv_d, EPS, ALU.mult, ALU.add, [ssb], [r.b])
        act(r.a[:nt, 0:n], r.a[:nt, 0:n], AF.Sqrt, [r.b], [r.b])
        S.op(DVE, lambda e: e.reciprocal(out=r.a[:nt, 0:n], in_=r.a[:nt, 0:n]), reads=[r.b], writes=[r.b])
        return r

    pkv = ExitStack()
    p1 = ExitStack()
    KVp = alloc(pkv, "KVp", [128, 3, NKP], BF16)
    KVso = alloc(pkv, "KVso", [128, 3, 64], BF16)

    if KSTOP == -1:
        return finish_now()
    NP = 32
    def pl(name): return alloc(p1, name, [128, NP])
    are = pl("are"); aim = pl("aim"); dtp = pl("dtp"); mag = pl("mag"); lr = pl("lr"); li = pl("li")
    fr = pl("fr"); fi = pl("fi"); t0 = pl("t0"); t1 = pl("t1"); t2 = pl("t2"); t3 = pl("t3")
    with nc.allow_non_contiguous_dma("small strided param loads"):
        for g2 in range(2):
            sl = slice(64 * g2, 64 * g2 + 64)
            S.dma("sp", are.a[sl, :], a_re.rearrange("(p g) n -> g n p", g=2)[g2], writes=[are.b])
            S.dma("sp", aim.a[sl, :], a_im.rearrange("(p g) n -> g n p", g=2)[g2], writes=[aim.b])
            S.dma("sp", dtp.a[sl, :], bass.AP(log_dt.tensor, g2, [[0, 64], [2, 32]]), writes=[dtp.b])
    act(dtp.a, dtp.a, AF.Exp, [dtp.b], [dtp.b])
    tt(DVE, t0.a, are.a, dtp.a, ALU.mult, [are.b, dtp.b], [t0.b])
    tt(DVE, t1.a, aim.a, dtp.a, ALU.mult, [aim.b, dtp.b], [t1.b])
    act(mag.a, t0.a, AF.Exp, [t0.b], [mag.b])
    c1 = pl("c1"); s1 = pl("s1")
    act(s1.a, t1.a, AF.Sin, [t1.b], [s1.b], scale=0.125)
    act(c1.a, t1.a, AF.Sin, [t1.b, halfpi.b], [c1.b], scale=-0.125, bias=halfpi.a)

    def csq(cr, ci):
        tt(DVE, t2.a, cr.a, cr.a, ALU.mult, [cr.b], [t2.b])
        tt(DVE, t3.a, ci.a, ci.a, ALU.mult, [ci.b], [t3.b])
        tt(DVE, ci.a, cr.a, ci.a, ALU.mult, [cr.b, ci.b], [ci.b])
        ts1(DVE, ci.a, ci.a, 2.0, ALU.mult, [ci.b], [ci.b])
        tt(DVE, cr.a, t2.a, t3.a, ALU.subtract, [t2.b, t3.b], [cr.b])
    for _ in range(3):
        csq(c1, s1)
    tt(DVE, lr.a, mag.a, c1.a, ALU.mult, [mag.b, c1.b], [lr.b])
    tt(DVE, li.a, mag.a, s1.a, ALU.mult, [mag.b, s1.b], [li.b])

    def cmul(outr, outi, ar_, ai_, br_, bi_, ta, tb):
        tt(DVE, ta[0], ar_[0], br_[0], ALU.mult, [ar_[1], br_[1]], [ta[1]])
        tt(DVE, tb[0], ai_[0], bi_[0], ALU.mult, [ai_[1], bi_[1]], [tb[1]])
        tt(DVE, outr[0], ta[0], tb[0], ALU.subtract, [ta[1], tb[1]], [outr[1]])
        tt(DVE, ta[0], ar_[0], bi_[0], ALU.mult, [ar_[1], bi_[1]], [ta[1]])
        tt(DVE, tb[0], ai_[0], br_[0], ALU.mult, [ai_[1], br_[1]], [tb[1]])
        tt(DVE, outi[0], ta[0], tb[0], ALU.add, [ta[1], tb[1]], [outi[1]])

    P = lambda t_: (t_.a, t_.b)
    den = pl("den"); lm1 = pl("lm1"); nai = pl("nai")
    tt(DVE, t2.a, are.a, are.a, ALU.mult, [are.b], [t2.b])
    tt(DVE, t3.a, aim.a, aim.a, ALU.mult, [aim.b], [t3.b])
    tt(DVE, den.a, t2.a, t3.a, ALU.add, [t2.b, t3.b], [den.b])
    S.op(DVE, lambda e: e.reciprocal(out=den.a, in_=den.a), reads=[den.b], writes=[den.b])
    ts1(DVE, lm1.a, lr.a, -1.0, ALU.add, [lr.b], [lm1.b])
    ts1(DVE, nai.a, aim.a, -1.0, ALU.mult, [aim.b], [nai.b])
    cmul(P(fr), P(fi), P(lm1), P(li), P(are), P(nai), P(t2), P(t3))
    tt(DVE, fr.a, fr.a, den.a, ALU.mult, [fr.b, den.b], [fr.b])
    tt(DVE, fi.a, fi.a, den.a, ALU.mult, [fi.b, den.b], [fi.b])
    Er = alloc(p1, "Er", [128, NP, TS]); Ei = alloc(p1, "Ei", [128, NP, TS]); dec0 = alloc(p1, "dec0", [128, NP, TS])
    tw0 = alloc(p1, "tw0", [128, NP, TS // 2]); tw1 = alloc(p1, "tw1", [128, NP, TS // 2])
    S.op(DVE, lambda e: e.memset(Er.a[:, :, 0:1], 1.0), writes=[Er.b])
    S.op(DVE, lambda e: e.memset(Ei.a[:, :, 0:1], 0.0), writes=[Ei.b])
    pr = pl("pr"); pi_ = pl("pi")
    cp(DVE, pr.a, c1.a, [c1.b], [pr.b]); cp(DVE, pi_.a, s1.a, [s1.b], [pi_.b])
    w = 1
    while w < TS:
        prb = bc(pr.a, [128, NP, w], 2); pib = bc(pi_.a, [128, NP, w], 2)
        a0 = Er.a[:, :, 0:w]; b0 = Ei.a[:, :, 0:w]
        ta = tw0.a[:, :, 0:w]; tb = tw1.a[:, :, 0:w]
        tt(DVE, ta, a0, prb, ALU.mult, [Er.b, pr.b], [tw0.b])
        tt(DVE, tb, b0, pib, ALU.mult, [Ei.b, pi_.b], [tw1.b])
        tt(DVE, Er.a[:, :, w:2 * w], ta, tb, ALU.subtract, [tw0.b, tw1.b], [Er.b])
        tt(DVE, ta, a0, pib, ALU.mult, [Er.b, pi_.b], [tw0.b])
        tt(DVE, tb, b0, prb, ALU.mult, [Ei.b, pr.b], [tw1.b])
        tt(DVE, Ei.a[:, :, w:2 * w], ta, tb, ALU.add, [tw0.b, tw1.b], [Ei.b])
        csq(pr, pi_)
        w *= 2
    cp(DVE, dec0.a, bc(mag.a, [128, NP, TS], 2), [mag.b], [dec0.b])
    S.op(DVE, lambda e: e.memset(dec0.a[:, :, 0:1], 0.0), writes=[dec0.b])
    LamT = {}; FT = {}
    for tv in (16, 32):
        Lr_ = pl(f"Lr{tv}"); Li_ = pl(f"Li{tv}"); Fr_ = pl(f"Fr{tv}"); Fi_ = pl(f"Fi{tv}")
        er = (Er.a[:, :, tv - 1], Er.b); ei = (Ei.a[:, :, tv - 1], Ei.b)
        cmul(P(Lr_), P(Li_), P(lr), P(li), er, ei, P(t2), P(t3))
        cmul(P(Fr_), P(Fi_), P(fr), P(fi), er, ei, P(t2), P(t3))
        LamT[tv] = (Lr_, Li_); FT[tv] = (Fr_, Fi_)
    LFr = pl("LFr"); LFi = pl("LFi"); nfi = pl("nfi")
    tt(DVE, t2.a, fr.a, fr.a, ALU.mult, [fr.b], [t2.b])
    tt(DVE, t3.a, fi.a, fi.a, ALU.mult, [fi.b], [t3.b])
    tt(DVE, den.a, t2.a, t3.a, ALU.add, [t2.b, t3.b], [den.b])
    S.op(DVE, lambda e: e.reciprocal(out=den.a, in_=den.a), reads=[den.b], writes=[den.b])
    ts1(DVE, nfi.a, fi.a, -1.0, ALU.mult, [fi.b], [nfi.b])
    cmul(P(LFr), P(LFi), P(lr), P(li), P(fr), P(nfi), P(t2), P(t3))
    tt(DVE, LFr.a, LFr.a, den.a, ALU.mult, [LFr.b, den.b], [LFr.b])
    tt(DVE, LFi.a, LFi.a, den.a, ALU.mult, [LFi.b, den.b], [LFi.b])
    flg = pl("flg"); S.dma("sp", flg.a, flagb, writes=[flg.b])

    if KSTOP == -2:
        return finish_now()
    BT = alloc(p1, "BT", [128, 8, 2, 128], BF16)
    CT = alloc(p1, "CT", [128, 8, 4, 2, 128], BF16)
    Dd = alloc(p1, "Dd", [128, 8, 128], BF16)
    with ExitStack() as su:
        Bn = [alloc(su, f"Bn{i}", [64, 64, 16]) for i in range(2)]
        Cn = [alloc(su, f"Cn{i}", [16, 64, 64]) for i in range(2)]
        Cl = [alloc(su, f"Cl{i}", [128, 32, 16]) for i in range(2)]
        Cp = [alloc(su, f"Cp{i}", [128, 32, 16]) for i in range(2)]
        ctmp = [alloc(su, f"ctmp{i}", [128, 32, 16]) for i in range(2)]
        mB = alloc(su, "mB", [128, 2]); mC = alloc(su, "mC", [128, 4, 8]); dcol = alloc(su, "dcol", [128, 8])
        S.dma("sp", mB.a, mask_b, writes=[mB.b])
        S.dma("sp", mC.a, mask_c.rearrange("p (a b) -> p a b", a=4), writes=[mC.b])
        with nc.allow_non_contiguous_dma("one-time ssm weight relayout"):
            S.dma("sp", dcol.a, ssm_d.rearrange("(t p) -> p t", p=128), writes=[dcol.b])
        for i, src in enumerate((b_re, b_im)):
            S.dma("sp", Bn[i].a, src.rearrange("g n c -> n g c"), writes=[Bn[i].b])
        for i, src in enumerate((c_re, c_im)):
            S.dma("sp", Cn[i].a, src.rearrange("g c n -> c g n"), writes=[Cn[i].b])
        for i in range(2):
            bp = psA[:, i, :].rearrange("p (t n) -> p t n", t=8)
            for t_ in range(8):
                S.op(PE, lambda e: e.transpose(bp[:, t_, :], Bn[i].a[:, 8 * t_:8 * t_ + 8, :].rearrange("p a b -> p (a b)"), identf.a[:64, :64]),
                     reads=[Bn[i].b, identf.b], writes=[pb[i]], inc=(t_ == 7))
            tt(DVE, BT.a[:, :, i, :].rearrange("p t (g n) -> p t g n", g=2), bc(bp, [128, 8, 2, 64], 2),
               mB.a.unsqueeze(1).unsqueeze(3).broadcast_to([128, 8, 2, 64]), ALU.mult, [pb[i], mB.b], [BT.b])
            cpp = psA[:, 2 + i, :].rearrange("p (t c) -> p t c", t=32)
            for tp_ in range(32):
                S.op(PE, lambda e: e.transpose(cpp[:, tp_, :], Cn[i].a[:, 2 * tp_:2 * tp_ + 2, :].rearrange("p a b -> p (a b)"), identf.a[:16, :16]),
                     reads=[Cn[i].b, identf.b], writes=[pb[2 + i]], inc=(tp_ == 31))
            cp(ACT, Cl[i].a, cpp, [pb[2 + i]], [Cl[i].b])
        frb = bc(fr.a, [128, 32, 16], 2); fib = bc(fi.a, [128, 32, 16], 2)
        tt(DVE, ctmp[0].a, Cl[0].a, frb, ALU.mult, [Cl[0].b, fr.b], [ctmp[0].b])
        tt(DVE, ctmp[1].a, Cl[1].a, fib, ALU.mult, [Cl[1].b, fi.b], [ctmp[1].b])
        tt(DVE, Cp[0].a, ctmp[0].a, ctmp[1].a, ALU.subtract, [ctmp[0].b, ctmp[1].b], [Cp[0].b])
        tt(DVE, ctmp[0].a, Cl[0].a, fib, ALU.mult, [Cl[0].b, fi.b], [ctmp[0].b])
        tt(DVE, ctmp[1].a, Cl[1].a, frb, ALU.mult, [Cl[1].b, fr.b], [ctmp[1].b])
        tt(DVE, Cp[1].a, ctmp[0].a, ctmp[1].a, ALU.add, [ctmp[0].b, ctmp[1].b], [Cp[1].b])
        ts1(DVE, Cp[1].a, Cp[1].a, -1.0, ALU.mult, [Cp[1].b], [Cp[1].b])
        for tl in range(8):
            for pln in range(2):
                tt(DVE, CT.a[:, tl, :, pln, :].rearrange("p a (g c) -> p a g c", g=8),
                   bc(Cp[pln].a[:, 4 * tl:4 * tl + 4, :], [128, 4, 8, 16], 2),
                   bc(mC.a, [128, 4, 8, 16], 3), ALU.mult, [Cp[pln].b, mC.b], [CT.b])
            ts1(DVE, Dd.a[:, tl, :], identf.a, dcol.a[:, tl:tl + 1], ALU.mult, [identf.b, dcol.b], [Dd.b])
        S.barrier()

    if KSTOP == 0:
        return finish_now()
    injr = pl("injr"); inji = pl("inji"); injmr = pl("injmr"); injmi = pl("injmi")
    hor = pl("hor"); hoi = pl("hoi"); h0r = pl("h0r"); h0i = pl("h0i")
    f2 = lambda ap: ap.rearrange("p a b -> p (a b)")
    from types import SimpleNamespace as NS
    NLANE = int(os.environ.get("KLANES", "2"))

    def run_lanes(factories, lanes, stagger):
        slots = [None] * len(lanes); delay = [stagger * i for i in range(len(lanes))]; idx = 0
        while True:
            busy = False
            for li in range(len(lanes)):
                if slots[li] is None and idx < len(factories):
                    if delay[li] > 0:
                        delay[li] -= 1; busy = True
                        continue
                    slots[li] = factories[idx](lanes[li]); idx += 1
                if slots[li] is not None:
                    busy = True
                    try:
                        next(slots[li])
                    except StopIteration:
                        slots[li] = None
            if not busy:
                break

    pw = ExitStack()
    g_mix = alloc(pw, "g_mix", [128, D]); g_q = alloc(pw, "g_q", [128, 384]); g_kv = alloc(pw, "g_kv", [128, 256])
    g_kr = alloc(pw, "g_kr", [128, 64]); g_qn = alloc(pw, "g_qn", [128, 128]); g_qr = alloc(pw, "g_qr", [128, 64])
    for t_, src, n in ((g_mix, norm_mix, D), (g_q, q_lora_norm, 384), (g_kv, kv_lora_norm, 256), (g_kr, k_rope_norm, 64),
                       (g_qn, q_nope_norm, 128), (g_qr, q_rope_norm, 64)):
        S.dma("sp", t_.a, bcast_row(src, n), writes=[t_.b])
    w_zs = alloc(pw, "w_zs", [128, 8, 704], BF16); w_u = alloc(pw, "w_u", [128, 8, 1024], BF16)
    w_q = alloc(pw, "w_q", [128, 3, 1536], BF16)
    w_in_v = w_in.rearrange("(k p) c -> p k c", p=128)
    S.dma("pool", w_zs.a, w_in_v[:, :, 0:704], writes=[w_zs.b])
    S.dma("pool", w_u.a, w_in_v[:, :, 704:1728], writes=[w_u.b])
    S.dma("pool", w_q.a, w_uq.rearrange("(k p) c -> p k c", p=128), writes=[w_q.b])
    pa = ExitStack()

    def mk_lane_a(li):
        A_ = lambda n, shp, dt=F32: alloc(pa, f"{n}_{li}", shp, dt)
        return NS(xt=A_("xt", [128, D]), junk=A_("junk", [128, 1536], BF16), ssq=A_("ssq", [128, 4]), xs=A_("xs", [128, D], BF16),
                  xnT=A_("xnT", [128, 8, 128], BF16), cq=A_("cq", [128, 384], BF16), ckv=A_("ckv", [128, 256]),
                  ckvb=A_("ckvb", [128, 256], BF16), krn=A_("krn", [128, 64]), kro=A_("kro", [128, 64]), krd=A_("krd", [128, 128], BF16),
                  rt=[A_(f"rt{i}", [128, 8, 32]) for i in range(4)], rope_t=A_("rope_t", [128, 64]), cqT=A_("cqT", [128, 3, 128], BF16),
                  sqq=A_("sqq", [128, 1536]), ssh=A_("ssh", [128, 16]), qtmp=A_("qtmp", [128, 8, 128]), qr=A_("qr", [128, 8, 64]),
                  Qn=A_("Qn", [128, 8, 128], BF16), Qrp=A_("Qrp", [128, 8, 64], BF16), QT=A_("QT", [128, 12, 128], BF16),
                  uT=A_("uT", [128, 8, 128], BF16))
    lanes_a = [mk_lane_a(li) for li in range(int(os.environ.get("KLA", NLANE)))]

    def rope_apply(L, dst_lo, dst_hi, src_lo, src_hi, cosb, sinb, R, W, shp):
        rt = L.rt
        a, b_, c_, d_ = [rt[i].a[:shp[0], :shp[1], :] if len(shp) == 3 else rt[i].a[:shp[0], 0, :] for i in range(4)]
        bs = [rt[i].b for i in range(4)]
        tt(DVE, a, src_lo, cosb, ALU.mult, R, [bs[0]])
        tt(DVE, b_, src_hi, sinb, ALU.mult, R, [bs[1]])
        tt(DVE, dst_lo, a, b_, ALU.subtract, [bs[0], bs[1]], W)
        tt(DVE, c_, src_hi, cosb, ALU.mult, R, [bs[2]])
        tt(DVE, d_, src_lo, sinb, ALU.mult, R, [bs[3]])
        tt(DVE, dst_hi, c_, d_, ALU.add, [bs[2], bs[3]], W)

    def token_block(L, kind, nt, src, rope_src, kvdst, lat_dst, kr_dst, fo, ucol, ucols):
        x = L.xt; sq_ = L.ssq; rp = L.rope_t; u = L.uT; ck_ = L.ckv; ko = L.kro; junk = L.junk; xs = L.xs; xnT = L.xnT
        cq = L.cq; ckvb = L.ckvb; krn = L.krn; krd = L.krd; cqT = L.cqT; sqq = L.sqq; ssh = L.ssh; qtmp = L.qtmp; qr = L.qr
        Qn = L.Qn; Qrp = L.Qrp; QT = L.QT
        S.dma("sp", x.a[:nt, :], src, writes=[x.b])
        S.dma("sp", rp.a[:nt, :], rope_src, writes=[rp.b])
        yield
        act(junk.a[:nt, 0:D], x.a[:nt, :], AF.Square, [x.b], [junk.b, sq_.b], accum_out=sq_.a[:nt, 0:1])
        r = rstd_of(sq_.a[:nt, 0:1], sq_.b, 1, 1.0 / D, nt)
        S.op(DVE, lambda e: e.scalar_tensor_tensor(out=xs.a[:nt, :], in0=x.a[:nt, :], scalar=r.a[:nt, 0:1], in1=g_mix.a[:nt, :],
                                                   op0=ALU.mult, op1=ALU.mult), reads=[x.b, r.b, g_mix.b], writes=[xs.b])
        yield
        pt = psT[:, 0, :].rearrange("p (k t) -> p k t", k=8)
        for k in range(8):
            S.op(PE, lambda e: e.transpose(pt[:, k, :nt], xs.a[:nt, 128 * k:128 * k + 128], ident.a[:nt, :nt]),
                 reads=[xs.b, ident.b], writes=[ptb[0]], inc=(k == 7))
        cp(ACT, xnT.a[:, :, :nt], pt[:, :, :nt], [ptb[0]], [xnT.b])
        yield
        zlo = 0 if kind != "ctx" else 384
        zs = psA[:, 0:2, :].rearrange("p a b -> p (a b)")
        for (ca, cb) in ((0, 512), (512, 704)):
            if cb <= zlo:
                continue
            ca2 = max(ca, zlo)
            for k in range(8):
                mm(zs[:nt, ca2:cb], xnT.a[:, k, :nt], w_zs.a[:, k, ca2:cb], k == 0, k == 7, [xnT.b, w_zs.b],
                   [pb[ca // 512]], inc=(k == 7))
        act(junk.a[:nt, 0:256], zs[:nt, 384:640], AF.Square, [pb[0], pb[1]], [junk.b, sq_.b], accum_out=sq_.a[:nt, 1:2])
        r = rstd_of(sq_.a[:nt, 1:2], sq_.b, 1, 1.0 / 256, nt)
        S.op(DVE, lambda e: e.scalar_tensor_tensor(out=ck_.a[:nt, :], in0=zs[:nt, 384:640], scalar=r.a[:nt, 0:1], in1=g_kv.a[:nt, :],
                                                   op0=ALU.mult, op1=ALU.mult), reads=[pb[0], pb[1], r.b, g_kv.b], writes=[ck_.b])
        cp(ACT, ckvb.a[:nt, :], ck_.a[:nt, :], [ck_.b], [ckvb.b])
        if lat_dst is not None:
            S.dma("sp", lat_dst[0].a[lat_dst[1]:lat_dst[1] + nt, :], ck_.a[:nt, :], reads=[ck_.b], writes=[lat_dst[0].b])
        act(junk.a[:nt, 0:64], zs[:nt, 640:704], AF.Square, [pb[1]], [junk.b, sq_.b], accum_out=sq_.a[:nt, 2:3])
        r = rstd_of(sq_.a[:nt, 2:3], sq_.b, 1, 1.0 / 64, nt)
        S.op(DVE, lambda e: e.scalar_tensor_tensor(out=krn.a[:nt, :], in0=zs[:nt, 640:704], scalar=r.a[:nt, 0:1], in1=g_kr.a[:nt, :],
                                                   op0=ALU.mult, op1=ALU.mult), reads=[pb[1], r.b, g_kr.b], writes=[krn.b])
        if kind != "ctx":
            act(junk.a[:nt, 0:384], zs[:nt, 0:384], AF.Square, [pb[0]], [junk.b, sq_.b], accum_out=sq_.a[:nt, 3:4])
            r = rstd_of(sq_.a[:nt, 3:4], sq_.b, 1, 1.0 / 384, nt)
            S.op(DVE, lambda e: e.scalar_tensor_tensor(out=cq.a[:nt, :], in0=zs[:nt, 0:384], scalar=r.a[:nt, 0:1], in1=g_q.a[:nt, :],
                                                       op0=ALU.mult, op1=ALU.mult), reads=[pb[0], r.b, g_q.b], writes=[cq.b])
        yield
        rope_apply(L, ko.a[:nt, 0:32], ko.a[:nt, 32:64], krn.a[:nt, 0:32], krn.a[:nt, 32:64], rp.a[:nt, 0:32], rp.a[:nt, 32:64],
                   [krn.b, rp.b], [ko.b], (nt, 32))
        if kr_dst is not None:
            S.dma("sp", kr_dst[0].a[kr_dst[1]:kr_dst[1] + nt, :], ko.a[:nt, :], reads=[ko.b], writes=[kr_dst[0].b])
        cp(ACT, krd.a[:nt, 0:64], ko.a[:nt, :], [ko.b], [krd.b])
        cp(ACT, krd.a[:nt, 64:128], ko.a[:nt, :], [ko.b], [krd.b])
        yield
        pt2 = psT[:, 1, :].rearrange("p (k t) -> p k t", k=8)
        S.op(PE, lambda e: e.transpose(pt2[:, 0, :nt], ckvb.a[:nt, 0:128], ident.a[:nt, :nt]), reads=[ckvb.b, ident.b], writes=[ptb[1]], inc=False)
        S.op(PE, lambda e: e.transpose(pt2[:, 1, :nt], ckvb.a[:nt, 128:256], ident.a[:nt, :nt]), reads=[ckvb.b, ident.b], writes=[ptb[1]], inc=False)
        S.op(PE, lambda e: e.transpose(pt2[:, 2, :nt], krd.a[:nt, :], ident.a[:nt, :nt]), reads=[krd.b, ident.b], writes=[ptb[1]],
             inc=(kind == "ctx"))
        if kind != "ctx":
            for k in range(3):
                S.op(PE, lambda e: e.transpose(pt2[:, 3 + k, :nt], cq.a[:nt, 128 * k:128 * k + 128], ident.a[:nt, :nt]),
                     reads=[cq.b, ident.b], writes=[ptb[1]], inc=(k == 2))
        for (kt, kc, ta_, tb_) in kvdst:
            cp(ACT, kt.a[:, :, kc:kc + (tb_ - ta_)], pt2[:, 0:3, ta_:tb_], [ptb[1]], [kt.b])
        if kind != "ctx":
            cp(ACT, cqT.a[:, :, :nt], pt2[:, 3:6, :nt], [ptb[1]], [cqT.b])
        yield
        if nt < 128:
            S.op(POOL, lambda e: e.memset(u.a, 0.0), writes=[u.b])
        up = psA[:, 4:6, :].rearrange("p a (m t) -> p (a m) t", m=4)
        for m in range(8):
            for k in range(8):
                mm(up[:, m, :nt], w_u.a[:, k, 128 * m:128 * m + 128], xnT.a[:, k, :nt], k == 0, k == 7, [w_u.b, xnT.b],
                   [pb[4 + m // 4]], inc=(k == 7))
        cp(ACT, u.a[:, :, :nt], up[:, :, :nt], [pb[4], pb[5]], [u.b])
        S.dma("sp", Us.a[:, :, ucol:ucol + ucols], u.a[:, :, :ucols], reads=[u.b], writes=[Us.b])
        yield
        if kind != "ctx":
            qp = psA[:, 1:4, :].rearrange("p a b -> p (a b)")
            for n in range(3):
                for k in range(3):
                    mm(qp[:nt, 512 * n:512 * n + 512], cqT.a[:, k, :nt], w_q.a[:, k, 512 * n:512 * n + 512], k == 0, k == 2,
                       [cqT.b, w_q.b], [pb[1 + n]], inc=(k == 2))
            qb = [pb[1], pb[2], pb[3]]
            qv = qp[:nt, :].rearrange("p (h d) -> p h d", h=8)
            act(sqq.a[:nt, :], qp[:nt, :], AF.Square, qb, [sqq.b])
            sv = sqq.a[:nt, :].rearrange("p (h d) -> p h d", h=8)
            S.op(DVE, lambda e: e.tensor_reduce(out=ssh.a[:nt, 0:8], in_=sv[:, :, 0:128], axis=AX.X, op=ALU.add), reads=[sqq.b], writes=[ssh.b])
            S.op(DVE, lambda e: e.tensor_reduce(out=ssh.a[:nt, 8:16], in_=sv[:, :, 128:192], axis=AX.X, op=ALU.add), reads=[sqq.b], writes=[ssh.b])
            ts1(DVE, ssh.a[:nt, 8:16], ssh.a[:nt, 8:16], 2.0, ALU.mult, [ssh.b], [ssh.b])
            r = rstd_of(ssh.a[:nt, 0:16], ssh.b, 16, 1.0 / 128, nt)
            tt(DVE, qtmp.a[:nt], qv[:, :, 0:128], bc(r.a[:nt, 0:8], [nt, 8, 128], 2), ALU.mult, qb + [r.b], [qtmp.b])
            tt(DVE, qr.a[:nt], qv[:, :, 128:192], bc(r.a[:nt, 8:16], [nt, 8, 64], 2), ALU.mult, qb + [r.b], [qr.b])
            yield
            tt(POOL, Qn.a[:nt], qtmp.a[:nt], bc(g_qn.a[:nt, :], [nt, 8, 128], 1), ALU.mult, [qtmp.b, g_qn.b], [Qn.b])
            tt(DVE, qr.a[:nt], qr.a[:nt], bc(g_qr.a[:nt, :], [nt, 8, 64], 1), ALU.mult, [qr.b, g_qr.b], [qr.b])
            cosb = bc(rp.a[:nt, 0:32], [nt, 8, 32], 1); sinb = bc(rp.a[:nt, 32:64], [nt, 8, 32], 1)
            rope_apply(L, Qrp.a[:nt, :, 0:32], Qrp.a[:nt, :, 32:64], qr.a[:nt, :, 0:32], qr.a[:nt, :, 32:64], cosb, sinb,
                       [qr.b, rp.b], [Qrp.b], (nt, 8, 32))
            yield
            for h in range(8):
                S.op(PE, lambda e: e.transpose(pt[:, h, :nt], Qn.a[:nt, h, :], ident.a[:nt, :nt]),
                     reads=[Qn.b, ident.b], writes=[ptb[0]], inc=(h == 7))
            cp(ACT, QT.a[:, 0:8, :nt], pt[:, :, :nt], [ptb[0]], [QT.b])
            for hp in range(4):
                S.op(PE, lambda e: e.transpose(pt2[:, hp, :nt], Qrp.a[:nt, 2 * hp:2 * hp + 2, :].rearrange("p a b -> p (a b)"), ident.a[:nt, :nt]),
                     reads=[Qrp.b, ident.b], writes=[ptb[1]], inc=(hp == 3))
            cp(ACT, QT.a[:, 8:12, :nt], pt2[:, 0:4, :nt], [ptb[1]], [QT.b])
            S.dma("sp", Qs.a[:, :, fo:fo + nt], QT.a[:, :, :nt], reads=[QT.b], writes=[Qs.b])
            yield

    UC_CTX, UC_OWN, UC_SMP = 32, 32 + 2048, 32 + 4096
    fa = [lambda L: token_block(L, "meta", 16, x_meta, rope_meta, [(KVp, 4096, 0, 16)], (lat_meta, 0), (kr_meta, 0), 2048, 0, 32)]
    for bi in range(16):
        fa.append(lambda L, bi=bi: token_block(L, "ctx", 128, x_ctx[128 * bi:128 * bi + 128, :], rope_ctx[128 * bi:128 * bi + 128, :],
                                               [(KVp, 128 * bi, 0, 128)], None, None, 0, UC_CTX + 128 * bi, 128))
    for bi in range(16):
        fa.append(lambda L, bi=bi: token_block(L, "own", 128, x_own[128 * bi:128 * bi + 128, :], rope_own[128 * bi:128 * bi + 128, :],
                                               [(KVp, 2048 + 128 * bi, 0, 128)], (lat_own, 128 * bi), (kr_own, 128 * bi), 128 * bi,
                                               UC_OWN + 128 * bi, 128))
    fa.append(lambda L: token_block(L, "smp", 64, x_smp, rope_smp, [(KVso, 0, 0, 64)], (lat_smp, 0), (kr_smp, 0), 2064, UC_SMP, 64))
    run_lanes(fa, lanes_a, 6)
    S.barrier()
    pa.close()
    pw.close()

    pbk = ExitStack()

    def mk_lane_b(li):
        A_ = lambda n, shp, dt=F32: alloc(pbk, f"{n}_{li}", shp, dt)
        return NS(Xr=A_("Xr", [128, NP, TS]), Xi=A_("Xi", [128, NP, TS]), tA=A_("tA", [128, NP, TS]), tB=A_("tB", [128, NP, TS]),
                  tC=A_("tC", [128, NP, TS]), tD=A_("tD", [128, NP, TS]), Hr=A_("Hr", [128, NP, TS], BF16), Hi=A_("Hi", [128, NP, TS], BF16),
                  ge=[A_(f"ge{i}", [128, 8, TS]) for i in range(3)], sb=A_("sb", [128, 8, TS], BF16), u=A_("u", [128, 8, TS], BF16))
    lanes_b = [mk_lane_b(li) for li in range(int(os.environ.get("KLB", "2")))]

    def final_state(B, tv, dst_re, dst_im):
        Fr_, Fi_ = FT[tv]
        gr = (B.Xr.a[:, :, tv - 1], B.Xr.b); gi = (B.Xi.a[:, :, tv - 1], B.Xi.b)
        cmul(P(hor), P(hoi), P(Fr_), P(Fi_), gr, gi, P(t2), P(t3))
        with nc.allow_non_contiguous_dma("ssm state out"):
            for g2 in range(2):
                sl = slice(64 * g2, 64 * g2 + 64)
                S.dma("sp", dst_re.rearrange("(p g) n -> g n p", g=2)[g2], hor.a[sl, :], reads=[hor.b])
                S.dma("sp", dst_im.rearrange("(p g) n -> g n p", g=2)[g2], hoi.a[sl, :], reads=[hoi.b])

    def ssm_sub(B, ucol, tv, full, fo, pre, post):
        Xr, Xi, tA, tB, tC, tD, Hr, Hi, ge, uT = B.Xr, B.Xi, B.tA, B.tB, B.tC, B.tD, B.Hr, B.Hi, B.ge, B.u
        S.dma("sp", uT.a, Us.a[:, :, ucol:ucol + TS], reads=[Us.b], writes=[uT.b])
        yield
        xb_ = [psA[:, 2 + pp, :].rearrange("p (a h t) -> p a h t", a=8, h=2) for pp in range(4)]
        for tl in range(8):
            for pln in range(2):
                for pp in range(4):
                    mm(xb_[pp][:, tl, pln, :], BT.a[32 * pp:32 * pp + 32, tl, pln, :], uT.a[32 * pp:32 * pp + 32, tl, :],
                       True, True, [BT.b, uT.b], [pb[2 + pp]], inc=(tl == 7 and pln == 1), tile_position=(32 * pp, 0))
        Xr4 = Xr.a.rearrange("p (a b) t -> p a b t", b=4); Xi4 = Xi.a.rearrange("p (a b) t -> p a b t", b=4)
        for pp in range(4):
            act(Xr4[:, :, pp, :], xb_[pp][:, :, 0, :], AF.Copy, [pb[2 + pp]], [Xr.b])
            act(Xi4[:, :, pp, :], xb_[pp][:, :, 1, :], AF.Copy, [pb[2 + pp]], [Xi.b])
        yield
        tt(DVE, tA.a, Er.a, Xr.a, ALU.mult, [Er.b, Xr.b], [tA.b])
        tt(DVE, tB.a, Ei.a, Xi.a, ALU.mult, [Ei.b, Xi.b], [tB.b])
        tt(DVE, tA.a, tA.a, tB.a, ALU.add, [tA.b, tB.b], [tA.b])
        tt(POOL, tC.a, Er.a, Xi.a, ALU.mult, [Er.b, Xi.b], [tC.b])
        tt(POOL, tD.a, Ei.a, Xr.a, ALU.mult, [Ei.b, Xr.b], [tD.b])
        tt(POOL, tC.a, tC.a, tD.a, ALU.subtract, [tC.b, tD.b], [tC.b])
        yield
        if pre is not None:
            pre(B)
        tt(DVE, tA.a[:, :, 0], tA.a[:, :, 0], injr.a, ALU.add, [tA.b, injr.b], [tA.b])
        tt(DVE, tC.a[:, :, 0], tC.a[:, :, 0], inji.a, ALU.add, [tC.b, inji.b], [tC.b])
        S.op(DVE, lambda e: e.tensor_tensor_scan(out=f2(Xr.a), data0=f2(dec0.a), data1=f2(tA.a), initial=0.0,
                                                 op0=ALU.mult, op1=ALU.add), reads=[dec0.b, tA.b], writes=[Xr.b])
        S.op(DVE, lambda e: e.tensor_tensor_scan(out=f2(Xi.a), data0=f2(dec0.a), data1=f2(tC.a), initial=0.0,
                                                 op0=ALU.mult, op1=ALU.add), reads=[dec0.b, tC.b], writes=[Xi.b])
        gr = (Xr.a[:, :, tv - 1], Xr.b); gi = (Xi.a[:, :, tv - 1], Xi.b)
        Lr_, Li_ = LamT[tv]
        cmul(P(injr), P(inji), P(Lr_), P(Li_), gr, gi, P(t2), P(t3))
        if post is not None:
            post(B)
        yield
        if not full:
            return
        tt(DVE, tA.a, Er.a, Xr.a, ALU.mult, [Er.b, Xr.b], [tA.b])
        tt(DVE, tB.a, Ei.a, Xi.a, ALU.mult, [Ei.b, Xi.b], [tB.b])
        tt(DVE, Hr.a, tA.a, tB.a, ALU.subtract, [tA.b, tB.b], [Hr.b])
        tt(POOL, tC.a, Ei.a, Xr.a, ALU.mult, [Ei.b, Xr.b], [tC.b])
        tt(POOL, tD.a, Er.a, Xi.a, ALU.mult, [Er.b, Xi.b], [tD.b])
        tt(POOL, Hi.a, tC.a, tD.a, ALU.add, [tC.b, tD.b], [Hi.b])
        yield
        bank = 1
        yv = psA[:, bank, 0:8 * TS].rearrange("p (a t) -> p a t", a=8)
        for tl in range(8):
            k = 0
            for pp in range(4):
                for pln, Hh in ((0, Hr), (1, Hi)):
                    mm(yv[:, tl, :], CT.a[:, tl, pp, pln, :], Hh.a[:, 4 * tl + pp, :], k == 0, False,
                       [CT.b, Hh.b], [pb[bank]], inc=False)
                    k += 1
            mm(yv[:, tl, :], Dd.a[:, tl, :], uT.a[:, tl, :], False, True, [Dd.b, uT.b], [pb[bank]], inc=(tl == 7))
        act(ge[0].a, yv, AF.Square, [pb[bank]], [ge[0].b])
        ts(DVE, ge[0].a, ge[0].a, 0.044715, 1.0, ALU.mult, ALU.add, [ge[0].b], [ge[0].b])
        tt(DVE, ge[1].a, ge[0].a, yv, ALU.mult, [ge[0].b, pb[bank]], [ge[1].b])
        act(ge[2].a, ge[1].a, AF.Sigmoid, [ge[1].b], [ge[2].b], scale=1.5957691216)
        tt(DVE, B.sb.a, ge[2].a, yv, ALU.mult, [ge[2].b, pb[bank]], [B.sb.b])
        S.dma("sp", Ss.a[:, :, fo:fo + tv], B.sb.a[:, :, :tv], reads=[B.sb.b], writes=[Ss.b])
        yield

    S.op(DVE, lambda e: e.memset(injr.a, 0.0), writes=[injr.b])
    S.op(DVE, lambda e: e.memset(inji.a, 0.0), writes=[inji.b])

    def post_meta(B):
        cp(DVE, injmr.a, injr.a, [injr.b], [injmr.b]); cp(DVE, injmi.a, inji.a, [inji.b], [injmi.b])

    def pre_blend(B):
        for (a_, m_) in ((injr, injmr), (inji, injmi)):
            tt(DVE, a_.a, a_.a, m_.a, ALU.subtract, [a_.b, m_.b], [a_.b])
            tt(DVE, a_.a, a_.a, flg.a, ALU.mult, [a_.b, flg.b], [a_.b])
            tt(DVE, a_.a, a_.a, m_.a, ALU.add, [a_.b, m_.b], [a_.b])

    def pre_smp(j):
        def f(B):
            with nc.allow_non_contiguous_dma("ssm state in"):
                for g2 in range(2):
                    sl = slice(64 * g2, 64 * g2 + 64)
                    S.dma("sp", h0r.a[sl, :], st_re[j].rearrange("(p g) n -> g n p", g=2)[g2], writes=[h0r.b])
                    S.dma("sp", h0i.a[sl, :], st_im[j].rearrange("(p g) n -> g n p", g=2)[g2], writes=[h0i.b])
            cmul(P(injr), P(inji), P(LFr), P(LFi), P(h0r), P(h0i), P(t2), P(t3))
        return f

    fb = [lambda B: ssm_sub(B, 0, 16, True, 2048, None, post_meta)]
    for s_ in range(64):
        fb.append(lambda B, s_=s_: ssm_sub(B, UC_CTX + 32 * s_, 32, False, 0, None, None))
    for s_ in range(64):
        fb.append(lambda B, s_=s_: ssm_sub(B, UC_OWN + 32 * s_, 32, True, 32 * s_, pre_blend if s_ == 0 else None,
                                           (lambda B: final_state(B, 32, so_re.a, so_im.a)) if s_ == 63 else None))
    for j in range(2):
        fb.append(lambda B, j=j: ssm_sub(B, UC_SMP + 32 * j, 32, True, 2064 + 32 * j, pre_smp(j),
                                         lambda B, j=j: final_state(B, 32, ss_re.a[j], ss_im.a[j])))
    run_lanes(fb, lanes_b, int(os.environ.get("KSTAG", "3")))
    S.barrier()
    pbk.close()
    if KSTOP == 1:
        return finish_now()
    p1.close()

    p2 = ExitStack()
    w_kv = alloc(p2, "w_kv", [128, 2, 2048], BF16)
    S.dma("pool", w_kv.a, w_ukv.rearrange("(k p) c -> p k c", p=128), writes=[w_kv.b])
    g_kn = alloc(p2, "g_kn", [128, 128]); S.dma("sp", g_kn.a, bcast_row(k_nope_norm, 128), writes=[g_kn.b])
    cbias = alloc(p2, "cbias", [128, 1]); S.dma("sp", cbias.a, ctx_bias, writes=[cbias.b])
    cmask = alloc(p2, "cmask", [128, 128], BF16); S.dma("pool", cmask.a, chunk_mask, writes=[cmask.b])
    KVs = [alloc(p2, f"KVs{j}", [128, 3, NKS], BF16) for j in range(2)]
    with ExitStack() as ld:
        ctk = [alloc(ld, f"ctk{i}", [128, 384], BF16) for i in range(2)]
        for j in range(2):
            for kb in range(17):
                nk = 128 if kb < 16 else 16
                c_ = ctk[kb % 2]
                if kb < 16:
                    S.dma("pool", c_.a[:, 0:256], cl[j, 128 * kb:128 * kb + 128, :], writes=[c_.b])
                    S.dma("pool", c_.a[:, 256:320], ck[j, 128 * kb:128 * kb + 128, :], writes=[c_.b])
                else:
                    S.dma("pool", c_.a[:16, 0:256], cml[j], writes=[c_.b])
                    S.dma("pool", c_.a[:16, 256:320], cmk[j], writes=[c_.b])
                cp(DVE, c_.a[:nk, 320:384], c_.a[:nk, 256:320], [c_.b], [c_.b])
                pt2 = psT[:, kb % 2, :].rearrange("p (k t) -> p k t", k=8)
                for k in range(3):
                    S.op(PE, lambda e: e.transpose(pt2[:, k, :nk], c_.a[:nk, 128 * k:128 * k + 128], ident.a[:nk, :nk]),
                         reads=[c_.b, ident.b], writes=[ptb[kb % 2]], inc=(k == 2))
                cp(ACT, KVs[j].a[:, :, 128 * kb:128 * kb + nk], pt2[:, 0:3, :nk], [ptb[kb % 2]], [KVs[j].b])
            cp(DVE, KVs[j].a[:, :, 2064:2096], KVso.a[:, :, 32 * j:32 * j + 32], [KVso.b], [KVs[j].b])
    KhT = [alloc(p2, f"KhT{i}", [128, NKP], BF16) for i in range(2)]
    Vh = [alloc(p2, f"Vh{i}", [128, 33, 128], BF16) for i in range(2)]
    KsT = [[alloc(p2, f"KsT{i}{j}", [128, NKS], BF16) for j in range(2)] for i in range(2)]
    Vs = [[alloc(p2, f"Vs{i}{j}", [128, 17, 128], BF16) for j in range(2)] for i in range(2)]
    Qh = [alloc(p2, f"Qh{i}", [128, 2, NF], BF16) for i in range(2)]
    ksq = alloc(p2, "ksq", [128, 4, 128]); kss = alloc(p2, "kss", [128, 4]); ktmp = alloc(p2, "ktmp", [128, 4, 128])
    knb = alloc(p2, "knb", [128, 4, 128], BF16)
    PT = [alloc(p2, f"PT{i}", [128, 512], BF16) for i in range(3)]
    rden = alloc(p2, "rden", [128, 512]); Oh = [alloc(p2, f"Oh{i}", [128, 512], BF16) for i in range(2)]
    pt_i = [0]
    tick = [lambda: None]

    xl = [NS(ksq=alloc(p2, f"xksq{i}", [128, 2, 128]), kss=alloc(p2, f"xkss{i}", [128, 2]), ktmp=alloc(p2, f"xktmp{i}", [128, 2, 128]),
             knb=alloc(p2, f"xknb{i}", [128, 2, 128], BF16), bank=i) for i in range(2)]

    def expand_group(L, KV, kT, Vd, grp, b0, h):
        ng = len(grp); nk = grp[0][1]
        kv = psA[:, L.bank, :].rearrange("p (b c) -> p b c", b=2)
        kb_ = [pb[L.bank]]
        for bi_, (c0, _) in enumerate(grp):
            for k in range(2):
                mm(kv[:nk, bi_, :], KV.a[:, k, c0:c0 + nk], w_kv.a[:, k, 256 * h:256 * h + 256], k == 0, k == 1,
                   [KV.b, w_kv.b], kb_, inc=(k == 1))
        cp(ACT, Vd.a[:nk, b0:b0 + ng, :], kv[:nk, :ng, 128:256], kb_, [Vd.b])
        act(L.ksq.a[:nk, :ng, :], kv[:nk, :ng, 0:128], AF.Square, kb_, [L.ksq.b])
        yield
        S.op(DVE, lambda e: e.tensor_reduce(out=L.kss.a[:nk, :ng], in_=L.ksq.a[:nk, :ng, :], axis=AX.X, op=ALU.add), reads=[L.ksq.b], writes=[L.kss.b])
        r = rstd_of(L.kss.a[:nk, :ng], L.kss.b, ng, 1.0 / 128, nk)
        tt(DVE, L.ktmp.a[:nk, :ng, :], kv[:nk, :ng, 0:128], bc(r.a[:nk, 0:ng], [nk, ng, 128], 2), ALU.mult, kb_ + [r.b], [L.ktmp.b])
        tt(POOL, L.knb.a[:nk, :ng, :], L.ktmp.a[:nk, :ng, :], bc(g_kn.a[:nk, :], [nk, ng, 128], 1), ALU.mult, [L.ktmp.b, g_kn.b], [L.knb.b])
        yield
        ptv = psT[:, 0, 0:512].rearrange("p (k t) -> p k t", k=4)[:, 2 * L.bank:2 * L.bank + 2, :]
        for bi_ in range(ng):
            S.op(PE, lambda e: e.transpose(ptv[:, bi_, :nk], L.knb.a[:nk, bi_, :], ident.a[:nk, :nk]),
                 reads=[L.knb.b, ident.b], writes=[ptb[0]], inc=(bi_ == ng - 1))
        if nk == 128:
            cp(ACT, kT.a[:, grp[0][0]:grp[0][0] + 128 * ng], ptv[:, :ng, :].rearrange("p k t -> p (k t)"), [ptb[0]], [kT.b])
        else:
            cp(ACT, kT.a[:, grp[0][0]:grp[0][0] + nk], ptv[:, 0, :nk], [ptb[0]], [kT.b])
        yield

    def expand_facts(KV, kT, Vd, keyblocks, h):
        out = []; gi = 0
        while gi < len(keyblocks):
            grp = [keyblocks[gi]]
            if keyblocks[gi][1] == 128 and gi + 1 < len(keyblocks) and keyblocks[gi + 1][1] == 128:
                grp.append(keyblocks[gi + 1])
            out.append(lambda L, grp=grp, b0=gi: expand_group(L, KV, kT, Vd, grp, b0, h))
            gi += len(grp)
        return out

    def expand(KV, kT, Vd, keyblocks, h):
        gi = 0
        while gi < len(keyblocks):
            grp = [keyblocks[gi]]
            while len(grp) < 4 and gi + len(grp) < len(keyblocks) and keyblocks[gi + len(grp)][1] == 128 and grp[0][1] == 128:
                grp.append(keyblocks[gi + len(grp)])
            ng = len(grp); nk = grp[0][1]; b0 = gi
            kv = psA[:, 0:2, :].rearrange("p a (b c) -> p (a b) c", b=2)
            for bi_, (c0, _) in enumerate(grp):
                for k in range(2):
                    mm(kv[:nk, bi_, :], KV.a[:, k, c0:c0 + nk], w_kv.a[:, k, 256 * h:256 * h + 256], k == 0, k == 1,
                       [KV.b, w_kv.b], [pb[bi_ // 2]], inc=(k == 1))
            kb_ = [pb[0], pb[1]]
            cp(ACT, Vd.a[:nk, b0:b0 + ng, :], kv[:nk, :ng, 128:256], kb_, [Vd.b])
            act(ksq.a[:nk, :ng, :], kv[:nk, :ng, 0:128], AF.Square, kb_, [ksq.b])
            S.op(DVE, lambda e: e.tensor_reduce(out=kss.a[:nk, :ng], in_=ksq.a[:nk, :ng, :], axis=AX.X, op=ALU.add), reads=[ksq.b], writes=[kss.b])
            r = rstd_of(kss.a[:nk, :ng], kss.b, ng, 1.0 / 128, nk)
            tt(DVE, ktmp.a[:nk, :ng, :], kv[:nk, :ng, 0:128], bc(r.a[:nk, 0:ng], [nk, ng, 128], 2), ALU.mult, kb_ + [r.b], [ktmp.b])
            tt(DVE, knb.a[:nk, :ng, :], ktmp.a[:nk, :ng, :], bc(g_kn.a[:nk, :], [nk, ng, 128], 1), ALU.mult, [ktmp.b, g_kn.b], [knb.b])
            ptv = psT[:, 0, 0:512].rearrange("p (k t) -> p k t", k=4)
            for bi_ in range(ng):
                S.op(PE, lambda e: e.transpose(ptv[:, bi_, :nk], knb.a[:nk, bi_, :], ident.a[:nk, :nk]),
                     reads=[knb.b, ident.b], writes=[ptb[0]], inc=(bi_ == ng - 1))
            if nk == 128:
                cp(ACT, kT.a[:, grp[0][0]:grp[0][0] + 128 * ng], ptv[:, :ng, :].rearrange("p k t -> p (k t)"), [ptb[0]], [kT.b])
            else:
                cp(ACT, kT.a[:, grp[0][0]:grp[0][0] + nk], ptv[:, 0, :nk], [ptb[0]], [kT.b])
            gi += ng
            yield

    def attend(h, Qt, q0, nq, kT, KV, Vd, blocks, dst_col):
        if os.environ.get("KNOATT"):
            return
        attend_(h, Qt, q0, nq, kT, KV, Vd, blocks, dst_col)

    def attend_(h, Qt, q0, nq, kT, KV, Vd, blocks, dst_col):
        hr = 64 * (h % 2)
        ob, db = 4, 5
        O = psA[:, ob, :]; Dn = psA[:, db, :]
        nb = len(blocks)
        sbanks = [(psA[:, 2, :], pb[2]), (psA[:, 3, :], pb[3]), (psX, pxb)]

        def emit_s(bi_):
            c0, nk, vb, qlo, bias, msk = blocks[bi_]
            Sps, sb_ = sbanks[bi_ % 3]
            mm(Sps[:nk, qlo:nq], kT.a[:, c0:c0 + nk], Qt.a[:, 0, q0 + qlo:q0 + nq], True, False, [kT.b, Qt.b], [sb_], inc=False)
            mm(Sps[:nk, qlo:nq], KV.a[hr:hr + 64, 2, c0:c0 + nk], Qt.a[hr:hr + 64, 1, q0 + qlo:q0 + nq], False, True,
               [KV.b, Qt.b], [sb_], inc=True)

        emit_s(0)
        if nb > 1:
            emit_s(1)
        for bi_, (c0, nk, vb, qlo, bias, msk) in enumerate(blocks):
            if bi_ + 2 < nb:
                emit_s(bi_ + 2)
            tick[0]()
            Sps, sb_ = sbanks[bi_ % 3]
            p_ = PT[pt_i[0] % 3]; pt_i[0] += 1
            if bias is not None:
                act(p_.a[:nk, qlo:nq], Sps[:nk, qlo:nq], AF.Exp, [sb_, bias[1]], [p_.b], scale=SCALE, bias=bias[0][:nk, :])
            else:
                act(p_.a[:nk, qlo:nq], Sps[:nk, qlo:nq], AF.Exp, [sb_], [p_.b], scale=SCALE)
            if msk:
                tt(POOL, p_.a[:nk, qlo:qlo + 128], p_.a[:nk, qlo:qlo + 128], cmask.a[:nk, :], ALU.mult, [p_.b, cmask.b], [p_.b])
            mm(O[:, qlo:nq], Vd.a[:nk, vb, :], p_.a[:nk, qlo:nq], bi_ == 0, bi_ == nb - 1, [Vd.b, p_.b], [pb[ob]], inc=False)
            mm(Dn[:, qlo:nq], ones.a[:nk, :], p_.a[:nk, qlo:nq], bi_ == 0, bi_ == nb - 1, [ones.b, p_.b], [pb[db]], inc=True)
        S.op(DVE, lambda e: e.reciprocal(out=rden.a[:, :nq], in_=Dn[:, :nq]), reads=[pb[db]], writes=[rden.b])
        o_ = Oh[h % 2]
        tt(DVE, o_.a[:, :nq], O[:, :nq], rden.a[:, :nq], ALU.mult, [pb[ob], rden.b], [o_.b])
        S.dma("sp", Os.a[:, h, dst_col:dst_col + nq], o_.a[:, :nq], reads=[o_.b], writes=[Os.b])

    pkeys = [(128 * i, 128) for i in range(32)] + [(4096, 16)]
    skeys = [(128 * i, 128) for i in range(16)] + [(2048, 48)]

    def prep_head(h):
        hb = h % 2
        S.dma("sp", Qh[hb].a[:, 0, :], Qs.a[:, h, :], reads=[Qs.b], writes=[Qh[hb].b])
        S.dma("sp", Qh[hb].a[:, 1, :], Qs.a[:, 8 + h // 2, :], reads=[Qs.b], writes=[Qh[hb].b])
        if os.environ.get("KXL"):
            fx = expand_facts(KVp, KhT[hb], Vh[hb], pkeys, h)
            for j in range(2):
                fx += expand_facts(KVs[j], KsT[hb][j], Vs[hb][j], skeys, h)
            run_lanes(fx, xl, 1)
            yield
        else:
            yield
            yield from expand(KVp, KhT[hb], Vh[hb], pkeys, h)
            for j in range(2):
                yield from expand(KVs[j], KsT[hb][j], Vs[hb][j], skeys, h)

    def drain(g):
        for _ in g:
            pass
    nxt = [None]
    cnt = [0]

    def do_tick():
        cnt[0] += 1
        if nxt[0] is not None and cnt[0] % 2 == 0:
            try:
                next(nxt[0])
            except StopIteration:
                nxt[0] = None
    tick[0] = do_tick
    drain(prep_head(0))
    for h in range(8):
        hb = h % 2
        if os.environ.get("KINT"):
            nxt[0] = prep_head(h + 1) if h < 7 else None
        elif h > 0:
            drain(prep_head(h))
        for sbi in range(4):
            blocks = [(128 * i, 128, i, 0, (cbias.a, cbias.b), False) for i in range(16)]
            blocks += [(4096, 16, 32, 0, None, False)]
            for ob_ in range(4 * sbi + 4):
                j_ = ob_ - 4 * sbi
                qlo = 128 * j_ if j_ > 0 else 0
                blocks.append((2048 + 128 * ob_, 128, 16 + ob_, qlo, None, j_ >= 0))
            attend(h, Qh[hb], 512 * sbi, 512, KhT[hb], KVp, Vh[hb], blocks, 512 * sbi)
        attend(h, Qh[hb], 2048, 16, KhT[hb], KVp, Vh[hb], [(4096, 16, 32, 0, None, False)], 2048)
        for j in range(2):
            blocks = [(c0, nk, i, 0, None, False) for i, (c0, nk) in enumerate(skeys)]
            attend(h, Qh[hb], 2064 + 32 * j, 32, KsT[hb][j], KVs[j], Vs[hb][j], blocks, 2064 + 32 * j)
        if nxt[0] is not None:
            drain(nxt[0])
            nxt[0] = None
    S.barrier()
    if KSTOP == 2:
        return finish_now()
    p2.close()
    pkv.close()

    p3 = ExitStack()
    g_mix3 = alloc(p3, "g_mix3", [128, D]); S.dma("sp", g_mix3.a, bcast_row(norm_mix, D), writes=[g_mix3.b])
    w_g = alloc(p3, "w_g", [128, 8, 2048], BF16); S.dma("pool", w_g.a, w_in_v[:, :, 1728:3776], writes=[w_g.b])
    w_oa = alloc(p3, "w_oa", [128, 8, D], BF16); S.dma("pool", w_oa.a, w_o.rearrange("(k p) c -> p k c", p=128), writes=[w_oa.b])
    w_gv = alloc(p3, "w_gv", [128, 8, D], BF16); S.dma("pool", w_gv.a, w_glu_v.rearrange("(k p) c -> p k c", p=128), writes=[w_gv.b])
    w_gg = alloc(p3, "w_gg", [128, 8, D], BF16); S.dma("pool", w_gg.a, w_glu_g.rearrange("(k p) c -> p k c", p=128), writes=[w_gg.b])
    w_ot = alloc(p3, "w_ot", [128, 8, D], BF16); S.dma("pool", w_ot.a, w_out.rearrange("(k p) c -> p k c", p=128), writes=[w_ot.b])
    xb3 = [alloc(p3, f"xb3{i}", [128, D]) for i in range(4)]
    junk3 = alloc(p3, "junk3", [128, D], BF16); ss3 = alloc(p3, "ss3", [128, 4]); xs3 = alloc(p3, "xs3", [128, D], BF16)
    xnT3 = alloc(p3, "xnT3", [128, 8, 512], BF16)
    OT3 = alloc(p3, "OT3", [128, 8, 512], BF16); ST3 = alloc(p3, "ST3", [128, 8, 512], BF16)
    mT = alloc(p3, "mT", [128, 8, 512], BF16)
    sga = alloc(p3, "sga", [128, 512]); sgg = alloc(p3, "sgg", [128, 512]); sgb = alloc(p3, "sgb", [128, 512])
    e1 = alloc(p3, "e1", [128, 512]); e2 = alloc(p3, "e2", [128, 512])
    h1t = [alloc(p3, f"h1t{i}", [128, D]) for i in range(2)]
    sbs = [([(x_own[512 * s_ + 128 * b_:512 * s_ + 128 * b_ + 128, :], 128) for b_ in range(4)], 512 * s_) for s_ in range(4)]
    sbs.append(([(x_meta, 16), (x_smp, 64)], 2048))
    for (blks, fo) in sbs:
        nsb = sum(n for _, n in blks)
        off = 0
        for bi_, (src, nt) in enumerate(blks):
            x = xb3[bi_]
            S.dma("sp", x.a[:nt, :], src, writes=[x.b])
            act(junk3.a[:nt, :], x.a[:nt, :], AF.Square, [x.b], [junk3.b, ss3.b], accum_out=ss3.a[:nt, bi_:bi_ + 1])
            r = rstd_of(ss3.a[:nt, bi_:bi_ + 1], ss3.b, 1, 1.0 / D, nt)
            S.op(DVE, lambda e: e.scalar_tensor_tensor(out=xs3.a[:nt, :], in0=x.a[:nt, :], scalar=r.a[:nt, 0:1], in1=g_mix3.a[:nt, :],
                                                       op0=ALU.mult, op1=ALU.mult), reads=[x.b, r.b, g_mix3.b], writes=[xs3.b])
            pt = psT[:, bi_ % 2, :].rearrange("p (k t) -> p k t", k=8)
            for k in range(8):
                S.op(PE, lambda e: e.transpose(pt[:, k, :nt], xs3.a[:nt, 128 * k:128 * k + 128], ident.a[:nt, :nt]),
                     reads=[xs3.b, ident.b], writes=[ptb[bi_ % 2]], inc=(k == 7))
            cp(ACT, xnT3.a[:, :, off:off + nt], pt[:, :, :nt], [ptb[bi_ % 2]], [xnT3.b])
            off += nt
        S.dma("sp", OT3.a[:, :, :nsb], Os.a[:, :, fo:fo + nsb], reads=[Os.b], writes=[OT3.b])
        S.dma("sp", ST3.a[:, :, :nsb], Ss.a[:, :, fo:fo + nsb], reads=[Ss.b], writes=[ST3.b])
        for m in range(8):
            cs = slice(128 * m, 128 * m + 128)
            specs = [(0, w_g, 0, xnT3), (1, w_g, 1024, xnT3), (2, w_oa, 0, OT3), (3, w_gv, 0, ST3), (4, w_gg, 0, ST3)]
            for (bk, wt, wo, at) in specs:
                for k in range(8):
                    mm(psA[:, bk, :nsb], wt.a[:, k, wo + 128 * m:wo + 128 * m + 128], at.a[:, k, :nsb], k == 0, k == 7,
                       [wt.b, at.b], [pb[bk]], inc=(k == 7))
            act(sga.a[:, :nsb], psA[:, 0, :nsb], AF.Sigmoid, [pb[0]], [sga.b])
            act(sgb.a[:, :nsb], psA[:, 1, :nsb], AF.Sigmoid, [pb[1]], [sgb.b])
            act(sgg.a[:, :nsb], psA[:, 4, :nsb], AF.Sigmoid, [pb[4]], [sgg.b])
            tt(DVE, e1.a[:, :nsb], sga.a[:, :nsb], psA[:, 2, :nsb], ALU.mult, [sga.b, pb[2]], [e1.b])
            tt(DVE, e2.a[:, :nsb], sgg.a[:, :nsb], psA[:, 3, :nsb], ALU.mult, [sgg.b, pb[3]], [e2.b])
            tt(POOL, e2.a[:, :nsb], e2.a[:, :nsb], sgb.a[:, :nsb], ALU.mult, [e2.b, sgb.b], [e2.b])
            tt(POOL, mT.a[:, m, :nsb], e1.a[:, :nsb], e2.a[:, :nsb], ALU.add, [e1.b, e2.b], [mT.b])
        off = 0
        for bi_, (src, nt) in enumerate(blks):
            hp = psA[:, 0:2, :].rearrange("p a b -> p (a b)") if bi_ % 2 == 0 else psA[:, 2:4, :].rearrange("p a b -> p (a b)")
            hb_ = [pb[0], pb[1]] if bi_ % 2 == 0 else [pb[2], pb[3]]
            for n in range(2):
                for k in range(8):
                    mm(hp[:nt, 512 * n:512 * n + 512], mT.a[:, k, off:off + nt], w_ot.a[:, k, 512 * n:512 * n + 512], k == 0, k == 7,
                       [mT.b, w_ot.b], [hb_[n]], inc=(k == 7))
            ht = h1t[bi_ % 2]
            tt(DVE, ht.a[:nt, :], hp[:nt, :], xb3[bi_].a[:nt, :], ALU.add, hb_ + [xb3[bi_].b], [ht.b])
            S.dma("sp", H1.a[fo + off:fo + off + nt, :], ht.a[:nt, :], reads=[ht.b], writes=[H1.b])
            off += nt
    S.barrier()
    if KSTOP == 3:
        return finish_now()
    p3.close()

    p4 = ExitStack()
    g_mlp = alloc(p4, "g_mlp", [128, D]); S.dma("sp", g_mlp.a, bcast_row(norm_mlp, D), writes=[g_mlp.b])
    w_upb = alloc(p4, "w_upb", [128, 8, 4096], BF16)
    w_dnb = alloc(p4, "w_dnb", [128, 32, D], BF16)
    w_up_v = w_up.rearrange("(k p) c -> p k c", p=128); w_dn_v = w_down.rearrange("(k p) c -> p k c", p=128)
    for q_ in range(4):
        S.dma("pool", w_upb.a[:, :, 1024 * q_:1024 * q_ + 1024], w_up_v[:, :, 1024 * q_:1024 * q_ + 1024], writes=[w_upb.b])
    for q_ in range(4):
        S.dma("pool", w_dnb.a[:, 8 * q_:8 * q_ + 8, :], w_dn_v[:, 8 * q_:8 * q_ + 8, :], writes=[w_dnb.b])
    hb4 = [alloc(p4, f"hb4{i}", [128, D]) for i in range(4)]
    junk4 = alloc(p4, "junk4", [128, D], BF16); ss4 = alloc(p4, "ss4", [128, 4]); xs4 = alloc(p4, "xs4", [128, D], BF16)
    xnT4 = alloc(p4, "xnT4", [128, 8, 256], BF16); a2T = alloc(p4, "a2T", [128, 32, 256], BF16)
    rl = [alloc(p4, f"rl{i}", [128, 256]) for i in range(2)]
    yt = [alloc(p4, f"yt{i}", [128, D]) for i in range(2)]
    sb4 = [[(256 * s_ + 128 * b_, 128, y_own, 256 * s_ + 128 * b_) for b_ in range(2)] for s_ in range(8)]
    sb4.append([(2064, 64, y_smp, 0)])
    it4 = [0]
    for blks in sb4:
        nsb = sum(b_[1] for b_ in blks)
        off = 0
        for bi_, (f0, nt, _, _) in enumerate(blks):
            hh = hb4[(it4[0] * 2 + bi_) % 4]
            S.dma("sp", hh.a[:nt, :], H1.a[f0:f0 + nt, :], reads=[H1.b], writes=[hh.b])
            act(junk4.a[:nt, :], hh.a[:nt, :], AF.Square, [hh.b], [junk4.b, ss4.b], accum_out=ss4.a[:nt, bi_:bi_ + 1])
            r = rstd_of(ss4.a[:nt, bi_:bi_ + 1], ss4.b, 1, 1.0 / D, nt)
            S.op(DVE, lambda e: e.scalar_tensor_tensor(out=xs4.a[:nt, :], in0=hh.a[:nt, :], scalar=r.a[:nt, 0:1], in1=g_mlp.a[:nt, :],
                                                       op0=ALU.mult, op1=ALU.mult), reads=[hh.b, r.b, g_mlp.b], writes=[xs4.b])
            pt = psT[:, bi_ % 2, :].rearrange("p (k t) -> p k t", k=8)
            for k in range(8):
                S.op(PE, lambda e: e.transpose(pt[:, k, :nt], xs4.a[:nt, 128 * k:128 * k + 128], ident.a[:nt, :nt]),
                     reads=[xs4.b, ident.b], writes=[ptb[bi_ % 2]], inc=(k == 7))
            cp(ACT, xnT4.a[:, :, off:off + nt], pt[:, :, :nt], [ptb[bi_ % 2]], [xnT4.b])
            off += nt
        for f in range(32):
            bk = 4 + f % 2
            for k in range(8):
                mm(psA[:, bk, :nsb], w_upb.a[:, k, 128 * f:128 * f + 128], xnT4.a[:, k, :nsb], k == 0, k == 7,
                   [w_upb.b, xnT4.b], [pb[bk]], inc=(k == 7))
            r_ = rl[f % 2]
            act(r_.a[:, :nsb], psA[:, bk, :nsb], AF.Relu, [pb[bk]], [r_.b])
            tt(DVE if f % 2 == 0 else POOL, a2T.a[:, f, :nsb], r_.a[:, :nsb], r_.a[:, :nsb], ALU.mult, [r_.b], [a2T.b])
        off = 0
        for bi_, (f0, nt, dst, drow) in enumerate(blks):
            hh = hb4[(it4[0] * 2 + bi_) % 4]
            yp = psA[:, 0:2, :].rearrange("p a b -> p (a b)") if bi_ % 2 == 0 else psA[:, 2:4, :].rearrange("p a b -> p (a b)")
            yb_ = [pb[0], pb[1]] if bi_ % 2 == 0 else [pb[2], pb[3]]
            for n in range(2):
                for f in range(32):
                    mm(yp[:nt, 512 * n:512 * n + 512], a2T.a[:, f, off:off + nt], w_dnb.a[:, f, 512 * n:512 * n + 512], f == 0, f == 31,
                       [a2T.b, w_dnb.b], [yb_[n]], inc=(f == 31))
            y_ = yt[bi_ % 2]
            tt(DVE, y_.a[:nt, :], yp[:nt, :], hh.a[:nt, :], ALU.add, yb_ + [hh.b], [y_.b])
            S.dma("sp", dst.a[drow:drow + nt, :], y_.a[:nt, :], reads=[y_.b], writes=[dst.b])
            off += nt
        it4[0] += 1
    S._wait(S.sp, [o.b for o in outs], [o.b for o in outs])
    S.barrier()
    p4.close()
    top.close()
    return nc


_NC = None


def _rope_table(pos):
    half = 32
    inv = (10000.0 ** (-np.arange(half, dtype=np.float32) / half)).astype(np.float32)
    ang = pos.astype(np.float32)[:, None] * inv[None, :]
    return np.concatenate([np.cos(ang), np.sin(ang)], axis=1).astype(np.float32)


def kernel(**inp):
    global _NC
    f = lambda a: np.ascontiguousarray(np.asarray(a, dtype=np.float32))
    if _NC is None:
        _NC = build_program()
    nc = _NC
    xp = f(inp["x_prompt"]); xsm = f(inp["x_sample"])
    shared = {
        "x_meta": f(inp["meta_tokens"]),
        "rope_meta": _rope_table(np.arange(16) - 16), "rope_ctx": _rope_table(np.arange(2048)),
        "rope_smp": np.tile(_rope_table(2048 + np.arange(32)), (2, 1)),
        "norm_mix": f(inp["norm_mix"]), "w_in": f(inp["w_in"][0]), "q_lora_norm": f(inp["q_lora_norm"]),
        "w_uq": f(inp["w_uq"][0]).reshape(384, 1536), "q_nope_norm": f(inp["q_nope_norm"]), "q_rope_norm": f(inp["q_rope_norm"]),
        "kv_lora_norm": f(inp["kv_lora_norm"]), "k_rope_norm": f(inp["k_rope_norm"]),
        "w_ukv": f(inp["w_ukv"][0]).reshape(256, 2048), "k_nope_norm": f(inp["k_nope_norm"]),
        "w_o_attn": f(inp["w_o_attn"][0]).reshape(1024, 1024),
        "ssm_a_re": f(inp["ssm_a_re"][0]), "ssm_a_im": f(inp["ssm_a_im"][0]), "ssm_log_dt": f(inp["ssm_log_dt"][0]),
        "ssm_b_re": f(inp["ssm_b_re"][0]), "ssm_b_im": f(inp["ssm_b_im"][0]),
        "ssm_c_re": f(inp["ssm_c_re"][0]), "ssm_c_im": f(inp["ssm_c_im"][0]), "ssm_d": f(inp["ssm_d"][0]),
        "w_glu_v": f(inp["w_glu_v"][0]), "w_glu_g": f(inp["w_glu_g"][0]), "w_out": f(inp["w_out"][0]),
        "norm_mlp": f(inp["norm_mlp"]), "w_mlp_up": f(inp["w_mlp_up"][0]), "w_mlp_down": f(inp["w_mlp_down"][0]),
    }
    pidx = np.arange(128)
    mask_b = np.zeros((128, 2), np.float32); mask_b[pidx, (pidx // 16) % 2] = 1.0
    mask_c = np.zeros((128, 4, 8), np.float32)
    for pp in range(4):
        mask_c[pidx, pp, 2 * pp + pidx // 64] = 1.0
    kk = np.arange(128)[:, None]; qq = np.arange(128)[None, :]
    chunk_mask = ((kk // 64) <= (qq // 64)).astype(np.float32)
    shared.update(mask_b=mask_b, mask_c=mask_c.reshape(128, 32), chunk_mask=chunk_mask)
    zeros_x = np.zeros((2048, 1024), np.float32)
    in_maps = []
    for c in range(8):
        b, half = c // 2, c % 2
        m = dict(shared)
        m["x_own"] = np.ascontiguousarray(xp[b, 2048 * half:2048 * half + 2048])
        m["x_ctx"] = np.ascontiguousarray(xp[b, 0:2048]) if half else zeros_x
        m["x_smp"] = np.ascontiguousarray(xsm[2 * c:2 * c + 2].reshape(64, 1024))
        m["cl"] = f(inp["cache_latent"][0, 2 * c:2 * c + 2]); m["cml"] = f(inp["cache_meta_latent"][0, 2 * c:2 * c + 2])
        m["ck"] = f(inp["cache_krope"][0, 2 * c:2 * c + 2]); m["cmk"] = f(inp["cache_meta_krope"][0, 2 * c:2 * c + 2])
        m["st_re"] = f(inp["state_ssm_re"][0, 2 * c:2 * c + 2]); m["st_im"] = f(inp["state_ssm_im"][0, 2 * c:2 * c + 2])
        m["rope_own"] = _rope_table(2048 * half + np.arange(2048))
        m["ctx_bias"] = np.full((128, 1), 0.0 if half else -30000.0, np.float32)
        m["flagb"] = np.full((128, 32), float(half), np.float32)
        in_maps.append(m)
    res = run_bass_kernel_spmd(nc, in_maps, core_ids=list(range(8)))
    R = res.results
    cat = lambda key: np.stack([np.concatenate([R[2 * b][key], R[2 * b + 1][key]], axis=0) for b in range(4)])
    y_prompt = cat("y_own")
    y_sample = np.concatenate([R[c]["y_smp"].reshape(2, 32, 1024) for c in range(8)], axis=0)
    lat_p = cat("lat_own")[None]; kr_p = cat("kr_own")[None]
    mlat = np.stack([R[2 * b]["lat_meta"] for b in range(4)])[None]
    mkr = np.stack([R[2 * b]["kr_meta"] for b in range(4)])[None]
    sre = np.stack([R[2 * b + 1]["so_re"] for b in range(4)])[None]
    sim = np.stack([R[2 * b + 1]["so_im"] for b in range(4)])[None]
    lat_s = np.concatenate([R[c]["lat_smp"].reshape(2, 32, 256) for c in range(8)], axis=0)[None]
    kr_s = np.concatenate([R[c]["kr_smp"].reshape(2, 32, 64) for c in range(8)], axis=0)[None]
    sre_s = np.concatenate([R[c]["ss_re"] for c in range(8)], axis=0)[None]
    sim_s = np.concatenate([R[c]["ss_im"] for c in range(8)], axis=0)[None]
    outs = (y_prompt, y_sample, lat_p, kr_p, mlat, mkr, sre, sim, lat_s, kr_s, sre_s, sim_s)
    return tuple(np.ascontiguousarray(o, dtype=np.float32) for o in outs)
```
